# Optimizing a Trainium2 kernel written in Bass

```python
import math
import jax
import jax.numpy as jnp
from jax import lax
import numpy as np

D_MODEL = 1024
BATCH = 16
SEQ = 4096
DEPTH = 1
DEC_BATCH = 8
DEC_SEQ = 16
PAST_LEN = 4096

CHUNK = 64
A_PREV_CHUNKS = 8
A_PAST = A_PREV_CHUNKS * CHUNK
A_HEADS = 8
A_HEAD_DIM = 64
REL_CLIP = 128
B_HEADS = 4
B_HEAD_DIM = 64
B_V_DIM = 2 * B_HEAD_DIM
M_HEADS = 4
M_HEAD_DIM = 128
N_MEM = 256
BRANCH_W = 512
N_BRANCH = 3
IN_SPLITS = (BRANCH_W,) * 10 + (N_BRANCH * D_MODEL,)
D_IN = 10 * BRANCH_W + N_BRANCH * D_MODEL
ROPE_THETA = 10000.0
RMS_EPS = 1e-6
Q_BLOCK = 128
NEG_INF = -1e30

kernel_name = 'hybrid_chunk_diff_mem_encoder_step'


def rmsnorm(x, g):
    xf = x.astype(jnp.float32)
    xf = xf * lax.rsqrt(jnp.mean(xf * xf, axis=-1, keepdims=True) + RMS_EPS)
    return (xf * g.astype(jnp.float32)).astype(x.dtype)


def rope(x, pos):
    d = x.shape[-1]
    inv = 1.0 / (ROPE_THETA ** (jnp.arange(0, d, 2, dtype=jnp.float32) / d))
    ang = pos.astype(jnp.float32)[:, None] * inv[None, :]
    cos = jnp.cos(ang)[:, None, :]
    sin = jnp.sin(ang)[:, None, :]
    xf = x.astype(jnp.float32)
    x1, x2 = xf[..., : d // 2], xf[..., d // 2:]
    return jnp.concatenate([x1 * cos - x2 * sin, x1 * sin + x2 * cos], axis=-1).astype(x.dtype)


def lambda_init(layer):
    return 0.8 - 0.6 * math.exp(-0.3 * layer)


def diff_lambda(lq1, lk1, lq2, lk2, lam_init):
    e = lambda a, b: jnp.exp(jnp.sum(a.astype(jnp.float32) * b.astype(jnp.float32)))
    return e(lq1, lk1) - e(lq2, lk2) + lam_init


def rel_bias_lookup(rel_bias, dist):
    idx = jnp.clip(dist, -REL_CLIP, REL_CLIP) + REL_CLIP
    return rel_bias[:, idx].astype(jnp.float32)


def project_in(x, pos, norm_in, w_in):
    b, s, _ = x.shape
    h = rmsnorm(x, norm_in)
    proj = jnp.einsum('bsd,de->bse', h, w_in)
    bounds = np.cumsum(IN_SPLITS)[:-1].tolist()
    aq, ak, av, az, bq, bk, bv, bz, mq, mz, g = jnp.split(proj, bounds, axis=-1)
    r = lambda a, nh, d: a.reshape(b, s, nh, d)
    return (r(aq, A_HEADS, A_HEAD_DIM), r(ak, A_HEADS, A_HEAD_DIM), r(av, A_HEADS, A_HEAD_DIM),
            rope(r(bq, 2 * B_HEADS, B_HEAD_DIM), pos), rope(r(bk, 2 * B_HEADS, B_HEAD_DIM), pos),
            r(bv, B_HEADS, B_V_DIM), r(mq, M_HEADS, M_HEAD_DIM), az, bz, mz, g)


def chunk_attn_prompt(q, k, v, rel_bias):
    b, s, h, d = q.shape
    n_chunks = s // CHUNK
    band = A_PAST + CHUNK
    kp = jnp.pad(k, ((0, 0), (A_PAST, 0), (0, 0), (0, 0)))
    vp = jnp.pad(v, ((0, 0), (A_PAST, 0), (0, 0), (0, 0)))
    qc = jnp.moveaxis(q.reshape(b, n_chunks, CHUNK, h, d), 1, 0)
    dist = jnp.arange(CHUNK)[:, None] + A_PAST - jnp.arange(band)[None, :]
    bias = rel_bias_lookup(rel_bias, dist)
    scale = d ** -0.5

    def one_chunk(args):
        c, qb = args
        start = c * CHUNK
        kb = lax.dynamic_slice_in_dim(kp, start, band, axis=1)
        vb = lax.dynamic_slice_in_dim(vp, start, band, axis=1)
        valid = (start - A_PAST + jnp.arange(band)) >= 0
        sc = jnp.einsum('bqhd,bkhd->bhqk', qb, kb).astype(jnp.float32) * scale + bias
        sc = jnp.where(valid, sc, NEG_INF)
        p = jax.nn.softmax(sc, axis=-1)
        return jnp.einsum('bhqk,bkhd->bqhd', p.astype(vb.dtype), vb)

    o = lax.map(one_chunk, (jnp.arange(n_chunks), qc))
    return jnp.moveaxis(o, 0, 1).reshape(b, s, h * d)


def chunk_attn_sample(q, k_new, v_new, cache_k, cache_v, rel_bias):
    b, t, h, d = q.shape
    p_len = cache_k.shape[1]
    k = jnp.concatenate([cache_k, k_new], axis=1)
    v = jnp.concatenate([cache_v, v_new], axis=1)
    kpos = jnp.concatenate([jnp.arange(p_len), p_len + jnp.arange(t)])
    dist = (p_len + jnp.arange(t))[:, None] - kpos[None, :]
    bias = rel_bias_lookup(rel_bias, dist)
    sc = jnp.einsum('bqhd,bkhd->bhqk', q, k).astype(jnp.float32) * (d ** -0.5) + bias
    p = jax.nn.softmax(sc, axis=-1)
    return jnp.einsum('bhqk,bkhd->bqhd', p.astype(v.dtype), v).reshape(b, t, h * d)


def diff_core(q, k, v, mask, lam, subln, lam_init):
    sc = jnp.einsum('bqhmd,bkhmd->bhmqk', q, k).astype(jnp.float32) * (B_HEAD_DIM ** -0.5)
    if mask is not None:
        sc = jnp.where(mask, sc, NEG_INF)
    p = jax.nn.softmax(sc, axis=-1)
    pd = p[:, :, 0] - lam * p[:, :, 1]
    o = jnp.einsum('bhqk,bkhe->bqhe', pd.astype(v.dtype), v)
    return rmsnorm(o, subln) * (1.0 - lam_init)


def diff_attn_prompt(q, k, v, lam, subln, lam_init):
    b, s = q.shape[:2]
    nq = s // Q_BLOCK
    qb = jnp.moveaxis(q.reshape(b, nq, Q_BLOCK, B_HEADS, 2, B_HEAD_DIM), 1, 0)
    k2 = k.reshape(b, s, B_HEADS, 2, B_HEAD_DIM)
    k_chunk = jnp.arange(s) // CHUNK

    def one_block(args):
        i, qblk = args
        q_chunk = (i * Q_BLOCK + jnp.arange(Q_BLOCK)) // CHUNK
        mask = k_chunk[None, :] <= q_chunk[:, None]
        return diff_core(qblk, k2, v, mask, lam, subln, lam_init)

    o = lax.map(one_block, (jnp.arange(nq), qb))
    return jnp.moveaxis(o, 0, 1).reshape(b, s, B_HEADS * B_V_DIM)


def diff_attn_sample(q, k_new, v_new, cache_k, cache_v, lam, subln, lam_init):
    b, t = q.shape[:2]
    k = jnp.concatenate([cache_k, k_new], axis=1)
    v = jnp.concatenate([cache_v, v_new], axis=1)
    n_k = k.shape[1]
    o = diff_core(q.reshape(b, t, B_HEADS, 2, B_HEAD_DIM), k.reshape(b, n_k, B_HEADS, 2, B_HEAD_DIM),
                  v, None, lam, subln, lam_init)
    return o.reshape(b, t, B_HEADS * B_V_DIM)


def memory_kv(mem, norm_mem, w_mem_kv):
    b, n, _ = mem.shape
    kv = jnp.einsum('bnd,de->bne', rmsnorm(mem, norm_mem), w_mem_kv)
    mk, mv = jnp.split(kv, 2, axis=-1)
    return mk.reshape(b, n, M_HEADS, M_HEAD_DIM), mv.reshape(b, n, M_HEADS, M_HEAD_DIM)


def mem_attn(q, mk, mv):
    b, s = q.shape[:2]
    sc = jnp.einsum('bqhd,bmhd->bhqm', q, mk).astype(jnp.float32) * (M_HEAD_DIM ** -0.5)
    p = jax.nn.softmax(sc, axis=-1)
    return jnp.einsum('bhqm,bmhd->bqhd', p.astype(mv.dtype), mv).reshape(b, s, M_HEADS * M_HEAD_DIM)


def merge_branches(o_a, z_a, o_b, z_b, o_m, z_m, g, w_ba, w_bb, w_bm, w_out):
    p_a = jnp.einsum('bse,ed->bsd', o_a * jax.nn.silu(z_a), w_ba)
    p_b = jnp.einsum('bse,ed->bsd', o_b * jax.nn.silu(z_b), w_bb)
    p_m = jnp.einsum('bse,ed->bsd', o_m * jax.nn.silu(z_m), w_bm)
    g_a, g_b, g_m = jnp.split(jax.nn.sigmoid(g), N_BRANCH, axis=-1)
    mixed = g_a * p_a + g_b * p_b + g_m * p_m
    return jnp.einsum('bsd,de->bse', mixed, w_out)


def setup_inputs(seed: int = 0) -> dict:
    key = jax.random.key(seed)
    ks = jax.random.split(key, 24)
    nrm = lambda k, shape, scale: jax.random.normal(k, shape, jnp.float32) * scale
    a_cache = min(A_PAST, PAST_LEN)
    return {
        'x_prompt': nrm(ks[0], (BATCH, SEQ, D_MODEL), 1.0),
        'x_sample': nrm(ks[1], (DEC_BATCH, DEC_SEQ, D_MODEL), 1.0),
        'cache_a_k': nrm(ks[2], (DEPTH, DEC_BATCH, a_cache, A_HEADS, A_HEAD_DIM), 1.0),
        'cache_a_v': nrm(ks[3], (DEPTH, DEC_BATCH, a_cache, A_HEADS, A_HEAD_DIM), 1.0),
        'cache_b_k': nrm(ks[4], (DEPTH, DEC_BATCH, PAST_LEN, 2 * B_HEADS, B_HEAD_DIM), 1.0),
        'cache_b_v': nrm(ks[5], (DEPTH, DEC_BATCH, PAST_LEN, B_HEADS, B_V_DIM), 1.0),
        'cache_mem_k': nrm(ks[6], (DEPTH, DEC_BATCH, N_MEM, M_HEADS, M_HEAD_DIM), 1.0),
        'cache_mem_v': nrm(ks[7], (DEPTH, DEC_BATCH, N_MEM, M_HEADS, M_HEAD_DIM), 1.0),
        'mem_prompt': nrm(ks[8], (BATCH, N_MEM, D_MODEL), 1.0),
        'norm_in': 1.0 + nrm(ks[9], (DEPTH, D_MODEL), 0.02),
        'w_in': nrm(ks[10], (DEPTH, D_MODEL, D_IN), D_MODEL ** -0.5),
        'rel_bias': nrm(ks[11], (DEPTH, A_HEADS, 2 * REL_CLIP + 1), 0.5),
        'lambda_q1': nrm(ks[12], (DEPTH, B_HEAD_DIM), 0.1),
        'lambda_k1': nrm(ks[13], (DEPTH, B_HEAD_DIM), 0.1),
        'lambda_q2': nrm(ks[14], (DEPTH, B_HEAD_DIM), 0.1),
        'lambda_k2': nrm(ks[15], (DEPTH, B_HEAD_DIM), 0.1),
        'subln': 1.0 + nrm(ks[16], (DEPTH, B_V_DIM), 0.02),
        'norm_mem': 1.0 + nrm(ks[17], (DEPTH, D_MODEL), 0.02),
        'w_mem_kv': nrm(ks[18], (DEPTH, D_MODEL, 2 * M_HEADS * M_HEAD_DIM), D_MODEL ** -0.5),
        'w_branch_a': nrm(ks[19], (DEPTH, BRANCH_W, D_MODEL), BRANCH_W ** -0.5),
        'w_branch_b': nrm(ks[20], (DEPTH, BRANCH_W, D_MODEL), BRANCH_W ** -0.5),
        'w_branch_m': nrm(ks[21], (DEPTH, BRANCH_W, D_MODEL), BRANCH_W ** -0.5),
        'w_out': nrm(ks[22], (DEPTH, D_MODEL, D_MODEL), D_MODEL ** -0.5),
        'norm_final': 1.0 + nrm(ks[23], (D_MODEL,), 0.02),
    }


def reference(x_prompt, x_sample, cache_a_k, cache_a_v, cache_b_k, cache_b_v, cache_mem_k, cache_mem_v,
              mem_prompt, norm_in, w_in, rel_bias, lambda_q1, lambda_k1, lambda_q2, lambda_k2, subln,
              norm_mem, w_mem_kv, w_branch_a, w_branch_b, w_branch_m, w_out, norm_final):
    hp, hs = x_prompt, x_sample
    s = x_prompt.shape[1]
    t = x_sample.shape[1]
    past = cache_b_k.shape[2]
    pos_p = jnp.arange(s)
    pos_s = past + jnp.arange(t)
    keep = min(A_PAST, s)
    new = [[] for _ in range(10)]
    for l in range(DEPTH):
        lam_init = lambda_init(l)
        lam = diff_lambda(lambda_q1[l], lambda_k1[l], lambda_q2[l], lambda_k2[l], lam_init)
        qa, ka, va, qb, kb, vb, qm, za, zb, zm, g = project_in(hp, pos_p, norm_in[l], w_in[l])
        o_a = chunk_attn_prompt(qa, ka, va, rel_bias[l])
        o_b = diff_attn_prompt(qb, kb, vb, lam, subln[l], lam_init)
        mk, mv = memory_kv(mem_prompt, norm_mem[l], w_mem_kv[l])
        o_m = mem_attn(qm, mk, mv)
        hp = hp + merge_branches(o_a, za, o_b, zb, o_m, zm, g, w_branch_a[l], w_branch_b[l], w_branch_m[l], w_out[l])
        new[0].append(ka[:, s - keep:])
        new[1].append(va[:, s - keep:])
        new[2].append(kb)
        new[3].append(vb)
        new[4].append(mk)
        new[5].append(mv)
        qa, ka, va, qb, kb, vb, qm, za, zb, zm, g = project_in(hs, pos_s, norm_in[l], w_in[l])
        o_a = chunk_attn_sample(qa, ka, va, cache_a_k[l], cache_a_v[l], rel_bias[l])
        o_b = diff_attn_sample(qb, kb, vb, cache_b_k[l], cache_b_v[l], lam, subln[l], lam_init)
        o_m = mem_attn(qm, cache_mem_k[l], cache_mem_v[l])
        hs = hs + merge_branches(o_a, za, o_b, zb, o_m, zm, g, w_branch_a[l], w_branch_b[l], w_branch_m[l], w_out[l])
        new[6].append(ka)
        new[7].append(va)
        new[8].append(kb)
        new[9].append(vb)
    y_prompt = rmsnorm(hp, norm_final)
    y_sample = rmsnorm(hs, norm_final)
    return (y_prompt, y_sample, jnp.stack(new[0]), jnp.stack(new[1]), jnp.stack(new[2]), jnp.stack(new[3]),
            jnp.stack(new[4]), jnp.stack(new[5]), jnp.stack(new[6]), jnp.stack(new[7]), jnp.stack(new[8]),
            jnp.stack(new[9]))
```

```python
import math
import numpy as np
import concourse.bass as bass
import concourse.mybir as mybir
from concourse.bass_utils import run_bass_kernel_spmd
from concourse.alu_op_type import AluOpType as ALU

AF = mybir.ActivationFunctionType
AX = mybir.AxisListType
F32 = mybir.dt.float32
BF16 = mybir.dt.bfloat16

NCORES = 8
S = 4096
D = 1024
TB = 512
NBLK = S // TB
EPS = 1e-6
LAM_INIT = 0.8 - 0.6 * math.exp(0.0)
ENGS = ('pe', 'act', 'dve', 'pool', 'sp')
COLS = dict(aq=0, ak=512, av=1024, az=1536, bq=2048, bk=2560, bv=3072, bz=3584, mq=4096, mz=4608)
GOFF = 5120
NPOS = S + 128


class Prog:
    def __init__(self, nc):
        self.nc = nc
        self.ins = []
        self.last_w = {}
        self.readers = {}

    def op(self, eng, fn, r=(), w=(), dsem=None, extra_deps=()):
        idx = len(self.ins)
        deps = set(extra_deps)
        for k in r:
            j = self.last_w.get(k)
            if j is not None:
                deps.add(j)
            if isinstance(k, tuple) and k[0] in ('ring', 'acc'):
                for j in self.readers.get(k, ()):
                    if self.ins[j]['eng'] != eng:
                        deps.add(j)
        for k in w:
            j = self.last_w.get(k)
            if j is not None:
                deps.add(j)
            for j in self.readers.get(k, ()):
                deps.add(j)
        for k in w:
            self.last_w[k] = idx
            self.readers[k] = []
        ws = set(w)
        for k in r:
            if k not in ws:
                lst = self.readers.setdefault(k, [])
                if dsem is None:
                    lst[:] = [j for j in lst if not (self.ins[j]['eng'] == eng and self.ins[j]['dsem'] is None)]
                lst.append(idx)
        deps.discard(idx)
        self.ins.append(dict(eng=eng, fn=fn, deps=deps, dsem=dsem, tag=getattr(self, 'tag', '')))
        return idx

    def emit(self):
        nc = self.nc
        ins = self.ins
        n = len(ins)
        is_target = [False] * n
        for i in ins:
            for j in i['deps']:
                is_target[j] = True
        cnt = {e: 0 for e in ENGS}
        comp = [None] * n
        dcount = {}
        for idx, i in enumerate(ins):
            if i['dsem'] is not None:
                k = i['dsem']
                dcount[k] = dcount.get(k, 0) + 1
                comp[idx] = (('d', k), 16 * dcount[k])
            elif is_target[idx]:
                cnt[i['eng']] += 1
                comp[idx] = (('e', i['eng']), cnt[i['eng']])
        for idx, i in enumerate(ins):
            if i['dsem'] is not None and str(i['dsem']).startswith('all:'):
                comp[idx] = (('d', i['dsem']), 16 * dcount[i['dsem']])
        sems = {}

        def getsem(key):
            if key not in sems:
                sems[key] = nc.alloc_semaphore('s%d' % len(sems))
            return sems[key]

        for e in ENGS:
            getsem(('e', e))
        streams = {e: [idx for idx in range(n) if ins[idx]['eng'] == e] for e in ENGS}
        self.nwaits = 0

        waits = [None] * n
        for e in ENGS:
            waited = {}
            for idx in streams[e]:
                i = ins[idx]
                need = {}
                for j in i['deps']:
                    sk, val = comp[j]
                    if sk == ('e', 'pe') and e == 'pe':
                        continue
                    if need.get(sk, 0) < val:
                        need[sk] = val
                wl = []
                for sk, val in need.items():
                    if waited.get(sk, 0) < val:
                        wl.append((sk, val))
                        waited[sk] = val
                        self.nwaits += 1
                waits[idx] = wl
        semv = {}
        pos = {e: 0 for e in ENGS}
        progress = True
        while progress:
            progress = False
            for e in ENGS:
                st = streams[e]
                while pos[e] < len(st):
                    idx = st[pos[e]]
                    if all(semv.get(sk, 0) >= val for sk, val in waits[idx]):
                        if comp[idx] is not None:
                            sk = comp[idx][0]
                            semv[sk] = semv.get(sk, 0) + (16 if sk[0] == 'd' else 1)
                        pos[e] += 1
                        progress = True
                    else:
                        break
        for e in ENGS:
            if pos[e] < len(streams[e]):
                idx = streams[e][pos[e]]
                raise RuntimeError("DEADLOCK: engine %s stuck at op %d waits %s have %s" % (
                    e, idx, waits[idx], [(sk, semv.get(sk, 0)) for sk, _ in waits[idx]]))

        def run(e, eng):
            for idx in streams[e]:
                i = ins[idx]
                for sk, val in waits[idx]:
                    eng.wait_ge(getsem(sk), val)
                r = i['fn'](eng)
                if r is not None:
                    i['name'] = r.ins.name
                if comp[idx] is not None:
                    sk, val = comp[idx]
                    assert r is not None, "target op emitted nothing"
                    r.then_inc(getsem(sk), 16 if sk[0] == 'd' else 1)

        with nc.Block() as block:
            @block.tensor
            def _(eng):
                run('pe', eng)

            @block.scalar
            def _(eng):
                run('act', eng)

            @block.vector
            def _(eng):
                run('dve', eng)

            @block.gpsimd
            def _(eng):
                run('pool', eng)

            @block.sync
            def _(eng):
                run('sp', eng)
        self.nsems = len(sems)


class Builder:
    def __init__(self, nc):
        self.nc = nc
        self.P = Prog(nc)
        self.out_dmas = []
        self.ring_i = 0
        self.w_i = 0
        self.x_i = 0
        self.stg_i = 0
        self.e_i = 0
        self.cs_i = 0
        self.dsem_n = 0

    def op(self, eng, fn, r=(), w=(), dsem=None):
        return self.P.op(eng, fn, r=r, w=w, dsem=dsem)

    def dma_in(self, out, in_, w, r=(), dsem=None, eng='sp', nc_ok=False):
        if dsem is None:
            dsem = 'all:p1'
        if nc_ok:
            return self.op(eng, lambda e: e.dma_start(out=out, in_=in_, allow_slow_non_contiguous=True), r=r, w=w, dsem=dsem)
        return self.op(eng, lambda e: e.dma_start(out=out, in_=in_), r=r, w=w, dsem=dsem)

    def dma_out(self, out, in_, r, dsem, eng='pool'):
        i = self.op(eng, lambda e: e.dma_start(out=out, in_=in_), r=r, dsem=dsem)
        self.out_dmas.append(i)
        return i

    def mm(self, out, lhsT, rhs, start, stop, r, w):
        return self.op('pe', lambda e: e.matmul(out, lhsT=lhsT, rhs=rhs, start=start, stop=stop, skip_group_check=True), r=r, w=w)

    def tr(self, out, in_, ident, r, w):
        return self.op('pe', lambda e: e.transpose(out=out, in_=in_, identity=ident), r=r, w=w)

    def act(self, out, in_, func, r, w, scale=None, bias=None, accum=None):
        kw = {}
        if scale is not None:
            kw['scale'] = scale
        if bias is not None:
            kw['bias'] = bias
        if accum is not None:
            kw['accum_out'] = accum
        return self.op('act', lambda e: e.activation(out=out, in_=in_, func=func, **kw), r=r, w=w)

    def tt(self, out, in0, in1, op, r, w, eng='dve'):
        return self.op(eng, lambda e: e.tensor_tensor(out=out, in0=in0, in1=in1, op=op), r=r, w=w)

    def ts(self, out, in0, s1, op0, r, w, s2=None, op1=None, eng='dve'):
        if op1 is None:
            return self.op(eng, lambda e: e.tensor_scalar(out=out, in0=in0, scalar1=s1, scalar2=None, op0=op0), r=r, w=w)
        return self.op(eng, lambda e: e.tensor_scalar(out=out, in0=in0, scalar1=s1, scalar2=s2, op0=op0, op1=op1), r=r, w=w)

    def stt(self, out, in0, scalar, in1, op0, op1, r, w):
        return self.op('dve', lambda e: e.scalar_tensor_tensor(out=out, in0=in0, scalar=scalar, in1=in1, op0=op0, op1=op1), r=r, w=w)

    def cp(self, out, in_, r, w, eng='dve'):
        if eng == 'act':
            return self.op('act', lambda e: e.activation(out=out, in_=in_, func=AF.Copy), r=r, w=w)
        return self.op(eng, lambda e: e.tensor_copy(out=out, in_=in_), r=r, w=w)

    def ring(self):
        b = self.ring_i % 5
        self.ring_i += 1
        return self.ringT[:, b * 512:(b + 1) * 512], ('ring', b)

    def declare(self):
        nc = self.nc
        di = lambda n, s: nc.dram_tensor(n, list(s), F32, kind="ExternalInput").ap()
        do = lambda n, s: nc.dram_tensor(n, list(s), F32, kind="ExternalOutput").ap()
        self.xp = di('xp', (2, S, D))
        self.xs = di('xs', (16, D))
        self.cak = di('cak', (512, 512))
        self.cav = di('cav', (512, 512))
        self.cbk = di('cbk', (S, 512))
        self.cbv = di('cbv', (S, 512))
        self.cmk = di('cmk', (256, 512))
        self.cmv = di('cmv', (256, 512))
        self.memp = di('memp', (2, 256, D))
        self.norm_in = di('norm_in', (D,))
        self.w_in = di('w_in', (D, 8192))
        self.rel_bias = di('rel_bias', (8, 257))
        self.lam4 = di('lam4', (256,))
        self.subln = di('subln', (128,))
        self.norm_mem = di('norm_mem', (D,))
        self.w_mem = di('w_mem', (D, 1024))
        self.w_br = [di('w_ba', (512, D)), di('w_bb', (512, D)), di('w_bm', (512, D))]
        self.w_out = di('w_out', (D, D))
        self.norm_final = di('norm_final', (D,))
        self.ident_d = di('ident', (128, 128))
        self.jrev_d = di('jrev', (128, 128))
        self.cs_d = di('cs', (NPOS, 64))
        self.yp = do('yp', (2, S, D))
        self.ys = do('ys', (16, D))
        self.nak_p = do('nak_p', (2, 512, 512))
        self.nav_p = do('nav_p', (2, 512, 512))
        self.nbk_p = do('nbk_p', (2, S, 512))
        self.nbv_p = do('nbv_p', (2, S, 512))
        self.nmk_p = do('nmk_p', (2, 256, 512))
        self.nmv_p = do('nmv_p', (2, 256, 512))
        self.nak_s = do('nak_s', (16, 512))
        self.nav_s = do('nav_s', (16, 512))
        self.nbk_s = do('nbk_s', (16, 512))
        self.nbv_s = do('nbv_s', (16, 512))
        sc = lambda n, L: nc.dram_tensor(n, [128, L], BF16, kind="Internal").ap()
        self.sc_win = {k: sc('sc_' + k, 4096) for k in COLS}
        self.sc_m = [sc('sc_m%d' % t, 4608) for t in range(8)]
        self.sc_wo = [sc('sc_wo%d' % h, 4096) for h in range(2)]
        self.sc_wm = [sc('sc_wm%d' % h, 4096) for h in range(2)]
        self.ext_d = nc.dram_tensor('ext', [8, 768], BF16, kind="Internal").ap()

        sb = lambda n, s, dt=F32: nc.alloc_sbuf_tensor(n, list(s), dt)
        self.ident_f = sb('ident_f', (128, 128))
        self.ident_b = sb('ident_b', (128, 128), BF16)
        self.jrev_b = sb('jrev_b', (128, 128), BF16)
        self.nin_b = sb('nin_b', (128, D))
        self.nfin = sb('nfin', (128, D))
        self.sublnc = sb('sublnc', (128, 128))
        self.l4 = sb('l4', (128, 256))
        self.lamt = sb('lamt', (128, 8))
        self.mhalf = sb('mhalf', (128, 16))
        self.stat = sb('stat', (128, 96))
        self.junk = sb('junk', (128, D), BF16)
        self.xr = [sb('xr%d' % i, (128, D)) for i in range(3)]
        self.hT = sb('hT', (128, 8, 512), BF16)
        self.wr = [sb('wr%d' % i, (128, 4608), BF16) for i in range(2)]
        self.bkT = sb('bkT', (128, 4, S + 128), BF16)
        self.bva = sb('bva', (128, 33, 4, 130), BF16)
        self.akT = sb('akT', (128, 4, 1024), BF16)
        self.ava = sb('ava', (128, 8, 8, 66), BF16)
        self.qT = sb('qT', (128, 12, 512), BF16)
        self.EBM = sb('EBM', (128, 8, 640), BF16)
        self.o = sb('o', (128, 4, 1536), BF16)
        self.mkT = [sb('mkT%d' % i, (128, 4, 256), BF16) for i in range(2)]
        self.mva = [sb('mva%d' % i, (128, 2, 4, 130), BF16) for i in range(2)]
        self.cst = [sb('cst%d' % i, (128, 4, 64)) for i in range(2)]
        self.ovA = sb('ovA', (128, 8, 512), BF16)
        self.ovB = sb('ovB', (128, 5, 512))
        self.ovC = sb('ovC', (128, 3, 512))
        self.accS = sb('accS', (128, 3, 387))
        self.ringT = nc.alloc_psum_tensor('ringT', [128, 5 * 512], F32)
        self.accT = nc.alloc_psum_tensor('accT', [128, 3 * 512], F32)

    def prologue_consts(self):
        nc = self.nc
        self.op('dve', lambda e: e.memset(self.accT[:, :], 1.0), w=[('acc', 0), ('acc', 1), ('acc', 2)])
        self.dma_in(self.ident_f[:], self.ident_d, w=['ident_f'])
        self.dma_in(self.ovB[:, 4, 0:128], self.jrev_d, w=[('ovB', 4)])
        self.dma_in(self.ovC[0:8, 0:2, :].rearrange("p a b -> p (a b)")[:, 0:256], self.rel_bias[:, 1:257], w=[('ovC', 0), ('ovC', 1)])
        self.cp(self.ident_b[:], self.ident_f[:], r=['ident_f'], w=['ident_b'])
        self.cp(self.jrev_b[:], self.ovB[:, 4, 0:128], r=[('ovB', 4)], w=['jrev_b'])
        self.dma_in(self.nin_b[:], self.norm_mem.partition_broadcast(128), w=['nin_b'], dsem='ninb')
        self.dma_in(self.nfin[:], self.norm_final.partition_broadcast(128), w=['nfin'])
        self.dma_in(self.sublnc[:], self.subln.partition_broadcast(128), w=['sublnc'])
        self.ts(self.sublnc[:], self.sublnc[:], 1.0 - LAM_INIT, ALU.mult, r=['sublnc'], w=['sublnc'])
        self.dma_in(self.l4[:], self.lam4.partition_broadcast(128), w=['l4'])
        self.op('dve', lambda e: e.memset(self.mhalf[:], -0.5), w=['mhalf'])
        l4 = self.l4
        lt = self.lamt
        self.tt(l4[:, 0:64], l4[:, 0:64], l4[:, 64:128], ALU.mult, r=['l4'], w=['l4'])
        self.tt(l4[:, 128:192], l4[:, 128:192], l4[:, 192:256], ALU.mult, r=['l4'], w=['l4'])
        self.op('dve', lambda e: e.tensor_reduce(out=lt[:, 0:2], in_=l4[:].rearrange("p (a b c) -> p a b c", a=2, b=2)[:, :, 0, :],
                                                 axis=AX.X, op=ALU.add), r=['l4'], w=['lamt'])
        self.act(lt[:, 2:4], lt[:, 0:2], AF.Exp, r=['lamt'], w=['lamt'])
        self.tt(lt[:, 4:5], lt[:, 2:3], lt[:, 3:4], ALU.subtract, r=['lamt'], w=['lamt'])
        self.ts(lt[:, 5:6], lt[:, 4:5], LAM_INIT, ALU.add, r=['lamt'], w=['lam'])
        self.lam = lt[:, 5:6]
        self.op('pool', lambda e: e.memset(self.bva[:, :, :, 128:130], 1.0), w=[('bva', j) for j in range(33)])
        self.op('pool', lambda e: e.memset(self.ava[:, :, :, 64:66], 1.0), w=[('ava', j) for j in range(8)])
        for i in range(2):
            self.op('pool', lambda e, i=i: e.memset(self.mva[i][:, :, :, 128:130], 1.0), w=[('mva', i)])

    def prologue_ebm(self):
        ex = self.ovC[:, 0:2, :].rearrange("p a b -> p (a b)")
        self.cp(ex[0:8, 256:768], ex[0:8, 255:256].to_broadcast([8, 512]), r=[('ovC', 0), ('ovC', 1)], w=[('ovC', 0), ('ovC', 1)])
        exb = self.junk
        self.act(exb[0:8, 0:768], ex[0:8, 0:768], AF.Exp, r=[('ovC', 0), ('ovC', 1)], w=['junk'])
        self.dma_in(self.ext_d, exb[0:8, 0:768], r=['junk'], w=['ext_d'], dsem='extst')
        qTf = self.qT[:, :, :].rearrange("p a b -> p (a b)")
        stgs = []
        for h in range(8):
            src = bass.AP(self.ext_d.tensor, h * 768, [[1, 128], [1, 640]])
            tmp = qTf[:, h * 640:(h + 1) * 640]
            keys = [('qT', c) for c in range((h * 640) // 512, ((h + 1) * 640 - 1) // 512 + 1)]
            self.dma_in(tmp, src, r=['ext_d'], w=[('ebs', h)], dsem='all:ebm')
            stgs.append((tmp, keys + [('ebs', h)]))
        for h in range(8):
            tmp, keys = stgs[h]
            for (c0, c1) in ((0, 512), (512, 640)):
                pt, pk = self.ring()
                self.mm(pt[:, 0:c1 - c0], self.jrev_b[:], tmp[:, c0:c1], True, True, r=keys + ['jrev_b'], w=[pk])
                self.cp(self.EBM[:, h, c0:c1], pt[:, 0:c1 - c0], r=[pk], w=['EBM'], eng=('act' if h % 2 else 'dve'))
        self.op('dve', lambda e: e.memset(self.EBM[64:128, :, 0:64], 0.0), r=[], w=['EBM'])
        self.op('dve', lambda e: e.memset(self.EBM[0:64, :, 4 * 128 + 64:4 * 128 + 128], 0.0), r=[], w=['EBM'])

    def cast_dma(self, dst, src, key):
        k = getattr(self, 'cast_n', 0)
        self.cast_n = k + 1
        hist = getattr(self, 'cast_hist', [])
        deps = [hist[k - 4]] if k >= 4 else []
        i = self.P.op('pool', lambda e: e.dma_start(out=dst, in_=src), w=[key], dsem='c%d' % (k % 4), extra_deps=deps)
        hist.append(i)
        self.cast_hist = hist

    def cast_item(self, name):
        w_in = self.w_in
        if name in COLS:
            c0 = COLS[name]
            dst = self.sc_win[name].rearrange("p (c f) -> p c f", c=8)
            src = w_in[:, c0:c0 + 512].rearrange("(c p) f -> p c f", p=128)
            self.cast_dma(dst, src, ('sc', name, 0))
        elif name[0] == 'm':
            t = int(name[1:])
            for b in range(3):
                dst = self.sc_m[t][:, 0:3072].rearrange("p (c b f) -> p c b f", c=8, b=3)[:, :, b, :]
                c0 = GOFF + b * 1024 + t * 128
                src = w_in[:, c0:c0 + 128].rearrange("(c p) f -> p c f", p=128)
                self.cast_dma(dst, src, ('sc', name, b))
            for b in range(3):
                dst = self.sc_m[t][:, 3072:4608].rearrange("p (b c f) -> p b c f", b=3, c=4)[:, b, :, :]
                src = self.w_br[b][:, t * 128:(t + 1) * 128].rearrange("(c p) f -> p c f", p=128)
                self.cast_dma(dst, src, ('sc', name, 3 + b))
        elif name[0:2] == 'wo':
            h = int(name[2:])
            dst = self.sc_wo[h].rearrange("p (c f) -> p c f", c=8)
            src = self.w_out[:, h * 512:(h + 1) * 512].rearrange("(c p) f -> p c f", p=128)
            self.cast_dma(dst, src, ('sc', name, 0))
        elif name[0:2] == 'wm':
            h = int(name[2:])
            dst = self.sc_wm[h].rearrange("p (c f) -> p c f", c=8)
            src = self.w_mem[:, h * 512:(h + 1) * 512].rearrange("(c p) f -> p c f", p=128)
            self.cast_dma(dst, src, ('sc', name, 0))

    def sc_ap(self, name):
        if name in COLS:
            return self.sc_win[name], 4096
        if name[0] == 'm' and name not in COLS:
            return self.sc_m[int(name[1:])], 4608
        if name[0:2] == 'wo':
            return self.sc_wo[int(name[2:])], 4096
        return self.sc_wm[int(name[2:])], 4096

    def load_w(self, name):
        slot = self.w_i % 2
        self.w_i += 1
        src, L = self.sc_ap(name)
        t = self.wr[slot]
        self.dma_in(t[:, 0:L], src, r=[('sc', name, b) for b in range(6)], w=[('w', slot)], dsem='w%d' % slot)
        return t, ('w', slot)

    def rstd(self, rows, a, b, n, ncols=1, tmpc=60):
        st = self.stat
        I32 = mybir.dt.int32
        x = st[0:rows, a:a + ncols]
        y = st[0:rows, b:b + ncols]
        t = st[0:rows, tmpc:tmpc + ncols]
        ka, kb, kt = ('stat', a), ('stat', b), ('stat', tmpc)
        self.ts(x, x, 1.0 / n, ALU.mult, s2=EPS, op1=ALU.add, r=[ka], w=[ka])
        self.ts(y.bitcast(I32), x.bitcast(I32), -0.5, ALU.mult, s2=float(0x5f3759df), op1=ALU.add, r=[ka], w=[kb])
        for it in range(3):
            if ncols == 1:
                self.stt(t, x, y, y, ALU.mult, ALU.mult, r=[ka, kb], w=[kt])
            else:
                self.tt(t, x, y, ALU.mult, r=[ka, kb], w=[kt])
                self.tt(t, t, y, ALU.mult, r=[kt, kb], w=[kt])
            self.ts(t, t, -0.5, ALU.mult, s2=1.5, op1=ALU.add, r=[kt], w=[kt])
            self.tt(y, y, t, ALU.mult, r=[kb, kt], w=[kb])

    def a1_load(self, src_tiles, i):
        src, rows = src_tiles[i]
        slot = self.x_i % 3
        self.x_i += 1
        xt = self.xr[slot]
        self.dma_in(xt[0:rows, :], src, w=[('x', slot)], dsem='x%d' % slot)
        self.a1_slots[i] = (slot, rows)

    def a1_comp(self, i):
        slot, rows = self.a1_slots[i]
        xt = self.xr[slot]
        xk = ('x', slot)
        sa = 2 * slot
        self.act(self.junk[0:rows, :], xt[0:rows, :], AF.Square, r=[xk], w=['junk', ('stat', sa)], accum=self.stat[0:rows, sa:sa + 1])
        self.rstd(rows, sa, sa + 1, D, tmpc=8 + slot)
        self.stt(self.o[0:rows, i, 0:1024], xt[0:rows, :], self.stat[0:rows, sa + 1:sa + 2], self.nin_b[0:rows, :], ALU.mult, ALU.mult,
                 r=[xk, ('stat', sa + 1), 'nin_b'], w=[('o', i, 0), ('o', i, 1)])

    def a1(self, src_tiles):
        self.a1_slots = {}
        for i in range(len(src_tiles)):
            self.a1_load(src_tiles, i)
            self.a1_comp(i)

    def a2(self, tiles_rows, hcol0=0):
        for i, rows in enumerate(tiles_rows):
            col = hcol0 + i * 128
            pt, pk = self.ring()
            ptb = pt.bitcast(BF16)
            for c in range(8):
                self.tr(ptb[:, c * 128:c * 128 + rows], self.o[0:rows, i, c * 128:(c + 1) * 128], self.ident_b[0:rows, 0:rows],
                        r=[('o', i, 0), ('o', i, 1), 'ident_b'], w=[pk])
            self.cp(self.hT[:, :, col:col + rows], ptb[:, :].rearrange("p (a b) -> p a b", a=8)[:, :, 0:rows], r=[pk], w=[('hT', col // 128)],
                    eng=('act' if i % 2 else 'dve'))

    def proj_tm(self, wt, wk, qi, rows, hcol0=0):
        pt, pk = self.ring()
        col = hcol0 + qi * 128
        for c in range(8):
            self.mm(pt[0:rows, :], self.hT[:, c, col:col + rows], wt[:, c * 512:(c + 1) * 512], c == 0, c == 7,
                    r=[('hT', col // 128), wk], w=[pk])
        return pt, pk

    def proj_fm(self, wt, wk, f, T, F=512, sub=None):
        pt, pk = self.ring()
        for c in range(8):
            if sub is None:
                l = wt[:, c * F + f * 128:c * F + (f + 1) * 128]
            else:
                l = wt[:, c * F + sub * 128:c * F + (sub + 1) * 128]
            self.mm(pt[:, 0:T], l, self.hT[:, c, 0:T], c == 0, c == 7,
                    r=[('hT', q) for q in range((T + 127) // 128)] + (wk if isinstance(wk, list) else [wk]), w=[pk])
        return pt, pk

    def stg(self):
        i = self.stg_i % 5
        self.stg_i += 1
        return self.ovB[:, i, :], ('ovB', i)

    def memory_kv(self):
        tiles = []
        for s in range(2):
            for t in range(2):
                tiles.append((self.memp[s, t * 128:(t + 1) * 128, :], 128))
        self.a1(tiles)
        self.a2([128] * 4)
        self.dma_in(self.nin_b[:], self.norm_in.partition_broadcast(128), w=['nin_b'], dsem='ninb')
        for half in range(2):
            wt, wk = self.load_w('wm%d' % half)
            for qi in range(4):
                s, t = qi // 2, qi % 2
                pt, pk = self.proj_tm(wt, wk, qi, 128)
                st, sk = self.stg()
                self.cp(st, pt, r=[pk], w=[sk], eng='act')
                dst = (self.nmk_p if half == 0 else self.nmv_p)[s, t * 128:(t + 1) * 128, :]
                self.dma_out(dst, st, r=[sk], dsem='o_' + sk[0] + str(sk[1]), eng='sp')
                if half == 1:
                    self.cp(self.mva[s][:, t, :, 0:128], st.rearrange("p (h d) -> p h d", h=4), r=[sk], w=[('mva', s)])
                else:
                    self.k_to_mkT(st, sk, s, t)

    def k_to_mkT(self, st, sk, s, t, rows=128):
        pt, pk = self.ring()
        for h in range(4):
            self.tr(pt[:, h * 128:h * 128 + rows], st[0:rows, h * 128:(h + 1) * 128], self.ident_f[0:rows, 0:rows], r=[sk, 'ident_f'], w=[pk])
        self.cp(self.mkT[s][:, :, t * 128:t * 128 + rows], pt[:, :].rearrange("p (a b) -> p a b", a=4)[:, :, 0:rows], r=[pk], w=[('mkT', s)])

    def rope(self, pt, pk, dst, dk, cs, ck, rows):
        t1 = self.ovC[0:rows, 2, 0:256].rearrange("p (m f) -> p m f", m=8)
        t2 = self.ovC[0:rows, 2, 256:512].rearrange("p (m f) -> p m f", m=8)
        tk = ('ovC', 2)
        xin = pt[0:rows, :].rearrange("p (m t f) -> p m t f", m=8, t=2)
        x1, x2 = xin[:, :, 0, :], xin[:, :, 1, :]
        dv = dst[0:rows, :].rearrange("p (m t f) -> p m t f", m=8, t=2)
        cos = cs[0:rows, 0:32].unsqueeze(1).to_broadcast([rows, 8, 32])
        sin = cs[0:rows, 32:64].unsqueeze(1).to_broadcast([rows, 8, 32])
        self.tt(t1, x1, cos, ALU.mult, r=[pk, ck], w=[tk])
        self.tt(t2, x2, sin, ALU.mult, r=[pk, ck], w=[tk])
        self.tt(dv[:, :, 0, :], t1, t2, ALU.subtract, r=[tk], w=[dk])
        self.tt(t1, x1, sin, ALU.mult, r=[pk, ck, dk], w=[tk])
        self.tt(t2, x2, cos, ALU.mult, r=[pk, ck], w=[tk])
        self.tt(dv[:, :, 1, :], t1, t2, ALU.add, r=[tk], w=[dk])

    def tm_to_fm(self, st, sk, dst_ap, dkeys, rows):
        pt, pk = self.ring()
        for f in range(4):
            self.tr(pt[:, f * 128:f * 128 + rows], st[0:rows, f * 128:(f + 1) * 128], self.ident_f[0:rows, 0:rows], r=[sk, 'ident_f'], w=[pk])
        self.cp(dst_ap, pt[:, :].rearrange("p (a b) -> p a b", a=4)[:, :, 0:rows], r=[pk], w=dkeys)

    def xsrc_of(self, cfg):
        if cfg['kind'] == 'p':
            s, b = cfg['s'], cfg['b']
            return [(self.xp[s, b * 512 + i * 128:b * 512 + (i + 1) * 128, :], 128) for i in range(4)]
        return [(self.xs, 16)]

    def block(self, cfg):
        kind = cfg['kind']
        if kind == 'p':
            s, b = cfg['s'], cfg['b']
            T = 512
            QT = [(i * 128, 128) for i in range(4)]
            pos0 = b * 512
            xsrc = [(self.xp[s, pos0 + i * 128:pos0 + (i + 1) * 128, :], 128) for i in range(4)]
            memslot = s
            last = (b == NBLK - 1)
            kt0 = 4 * b
        else:
            T = 16
            QT = [(0, 16)]
            pos0 = S
            xsrc = [(self.xs, 16)]
            memslot = 0
            last = True
            kt0 = 32
            b = None
        NQ = len(QT)
        hk = [('hT', q) for q in range(NQ)]
        csi = self.cs_i % 2
        self.cs_i += 1
        cst = self.cst[csi]
        ck = ('cs', csi)
        if kind == 'p':
            self.dma_in(cst[:], self.cs_d[pos0:pos0 + 512, :].rearrange("(t p) f -> p t f", p=128), w=[ck], dsem='cs%d' % csi)
        else:
            self.dma_in(cst[0:16, 0, :], self.cs_d[pos0:pos0 + 16, :], w=[ck], dsem='cs%d' % csi)
        if not cfg.get('a2_done'):
            self.P.tag = 'A'
            self.a2([r_ for (_, r_) in QT])

        out_bk = (lambda qi: self.nbk_p[s, pos0 + qi * 128:pos0 + (qi + 1) * 128, :]) if kind == 'p' else (lambda qi: self.nbk_s)
        out_bv = (lambda qi: self.nbv_p[s, pos0 + qi * 128:pos0 + (qi + 1) * 128, :]) if kind == 'p' else (lambda qi: self.nbv_s)
        out_ak = (lambda qi: self.nak_p[s, qi * 128:(qi + 1) * 128, :]) if kind == 'p' else (lambda qi: self.nak_s)
        out_av = (lambda qi: self.nav_p[s, qi * 128:(qi + 1) * 128, :]) if kind == 'p' else (lambda qi: self.nav_s)
        qT = self.qT
        deferred = []
        for name in ('aq', 'ak', 'mq', 'av', 'bv', 'bq', 'bk'):
            self.P.tag = 'B_' + name
            wt, wk = self.load_w(name)
            if name in ('bq', 'bk'):
                for qi, (q0, rows) in enumerate(QT):
                    pt, pk = self.proj_tm(wt, wk, qi, rows)
                    st, sk = self.stg()
                    self.rope(pt, pk, st, sk, cst[:, qi, :], ck, rows)
                    while len(deferred) > 0:
                        deferred.pop(0)()
                    if name == 'bk':
                        self.dma_out(out_bk(qi), st[0:rows, :], r=[sk], dsem='o_' + sk[0] + str(sk[1]))
                        j = kt0 + qi
                        deferred.append(lambda st=st, sk=sk, j=j, rows=rows: self.tm_to_fm(st, sk, self.bkT[:, :, j * 128:j * 128 + rows], [('bkT', j)], rows))
                    else:
                        deferred.append(lambda st=st, sk=sk, q0=q0, rows=rows: self.tm_to_fm(st, sk, qT[:, 4:8, q0:q0 + rows], [('qT', 4 + f) for f in range(4)], rows))
            elif name == 'bv':
                for qi, (q0, rows) in enumerate(QT):
                    pt, pk = self.proj_tm(wt, wk, qi, rows)
                    st, sk = self.stg()
                    self.cp(st[0:rows, :], pt[0:rows, :], r=[pk], w=[sk], eng='act')
                    self.dma_out(out_bv(qi), st[0:rows, :], r=[sk], dsem='o_' + sk[0] + str(sk[1]))
                    j = kt0 + qi
                    self.cp(self.bva[0:rows, j, :, 0:128], st[0:rows, :].rearrange("p (h d) -> p h d", h=4), r=[sk], w=[('bva', j)], eng='act')
            elif name == 'av':
                for qi, (q0, rows) in enumerate(QT):
                    pt, pk = self.proj_tm(wt, wk, qi, rows)
                    slot = (kt0 + qi) % 8 if kind == 'p' else 4
                    self.cp(self.ava[0:rows, slot, :, 0:64], pt[0:rows, :].rearrange("p (h d) -> p h d", h=8), r=[pk], w=[('ava', slot)], eng='act')
                    if last:
                        st, sk = self.stg()
                        self.cp(st[0:rows, :], pt[0:rows, :], r=[pk], w=[sk], eng='act')
                        self.dma_out(out_av(qi), st[0:rows, :], r=[sk], dsem='o_' + sk[0] + str(sk[1]))
            elif name in ('aq', 'mq'):
                base = 0 if name == 'aq' else 8
                for f in range(4):
                    pt, pk = self.proj_fm(wt, wk, f, T)
                    self.cp(qT[:, base + f, 0:T], pt[:, 0:T], r=[pk], w=[('qT', base + f)], eng='act')
            elif name == 'ak':
                for f in range(4):
                    pt, pk = self.proj_fm(wt, wk, f, T)
                    if kind == 'p':
                        c0 = (kt0 % 8) * 128
                        keys = [('akT', (kt0 + q) % 8) for q in range(4)]
                    else:
                        c0 = 4 * 128
                        keys = [('akT', 4)]
                    self.cp(self.akT[:, f, c0:c0 + T], pt[:, 0:T], r=[pk], w=keys, eng='act')
                if last:
                    for qi, (q0, rows) in enumerate(QT):
                        pt, pk = self.proj_tm(wt, wk, qi, rows)
                        st, sk = self.stg()
                        self.cp(st[0:rows, :], pt[0:rows, :], r=[pk], w=[sk], eng='act')
                        self.dma_out(out_ak(qi), st[0:rows, :], r=[sk], dsem='o_' + sk[0] + str(sk[1]))

        if self.late_casts:
            self.P.tag = 'cast'
            for nm in self.late_casts:
                self.cast_item(nm)
            self.late_casts = []
        nxt = cfg.get('next')
        if nxt is not None:
            self.P.tag = 'A1'
            nsrc = self.xsrc_of(nxt)
            self.a1_slots = {}
            for i in range(min(3, len(nsrc))):
                self.a1_load(nsrc, i)
        self.P.tag = 'attA'
        self.attn_A(kind, b, QT, kt0)
        while len(deferred) > 0:
            deferred.pop(0)()
        self.P.tag = 'Ez'
        azbuf = self.ovB[:, :, :].rearrange("p a b -> p (a b)").bitcast(BF16)
        azk = [('ovB', i) for i in range(4)]
        src_az, L_az = self.sc_ap('az')
        self.dma_in(azbuf[:, 0:L_az], src_az, r=[('sc', 'az', b_) for b_ in range(6)], w=azk, dsem='azp')
        self.P.tag = 'attB'
        self.attn_B(kind, b, QT, kt0)
        self.P.tag = 'attM'
        self.attn_M(QT, T, memslot)
        self.P.tag = 'D'

        for qi, (q0, rows) in enumerate(QT):
            for g in range(3):
                pt, pk = self.ring()
                ptb = pt.bitcast(BF16)
                for cc in range(4):
                    c = g * 4 + cc
                    self.tr(ptb[:, cc * 128:cc * 128 + rows], self.o[0:rows, qi, c * 128:(c + 1) * 128], self.ident_b[0:rows, 0:rows],
                            r=[('o', qi, g), 'ident_b'], w=[pk])
                self.cp(qT[:, g * 4:(g + 1) * 4, q0:q0 + rows], ptb[:, 0:512].rearrange("p (a b) -> p a b", a=4)[:, :, 0:rows],
                        r=[pk], w=[('qT', g * 4 + cc) for cc in range(4)])

        self.P.tag = 'Ez'
        for g, name in enumerate(('az', 'bz', 'mz')):
            if name == 'az':
                wt, wk = azbuf, azk
            else:
                wt, wk = self.load_w(name)
            for f in range(4):
                c = g * 4 + f
                pt, pk = self.proj_fm(wt, wk, f, T)
                tz, tzk = self.ovC[:, 0, 0:T], ('ovC', 0)
                sz, szk = self.ovC[:, 1, 0:T], ('ovC', 1)
                self.act(tz, pt[:, 0:T], AF.Tanh, r=[pk], w=[tzk], scale=0.5)
                self.stt(sz, tz, 1.0, pt[:, 0:T], ALU.add, ALU.mult, r=[tzk, pk], w=[szk])
                self.stt(qT[:, c, 0:T], sz, 0.5, qT[:, c, 0:T], ALU.mult, ALU.mult, r=[szk, ('qT', c)], w=[('qT', c)])
            if nxt is not None and g == 0:
                self.P.tag = 'A1'
                self.a1_comp(0)
                self.P.tag = 'Ez'
        if nxt is not None and len(nsrc) > 3:
            self.P.tag = 'A1'
            self.a1_load(nsrc, 3)
        self.P.tag = 'Eg'
        mixT = self.ovA
        thv = self.ovB[:, 0:3, :].bitcast(BF16)
        for t in range(8):
            mt_, gk = self.load_w('m%d' % t)
            gt = mt_[:, 0:3072]
            wbt, wbk = mt_[:, 3072:4608], gk
            ths = []
            for br in range(3):
                pt, pk = self.proj_fm(gt, gk, None, T, F=384, sub=br)
                slot = (t % 2) * 3 + br
                th = thv[:, slot // 2, (slot % 2) * 512:(slot % 2) * 512 + T]
                thk = ('ovB', slot // 2)
                self.act(th, pt[:, 0:T], AF.Tanh, r=[pk], w=[thk], scale=0.5)
                ths.append((th, thk))
            mts = []
            for br in range(3):
                pt, pk = self.ring()
                for cc in range(4):
                    self.mm(pt[:, 0:T], wbt[:, (br * 4 + cc) * 128:(br * 4 + cc + 1) * 128], qT[:, br * 4 + cc, 0:T], cc == 0, cc == 3,
                            r=[wbk, ('qT', br * 4 + cc)], w=[pk])
                mt, mk = self.ovC[:, br, 0:T], ('ovC', br)
                self.stt(mt, ths[br][0], 1.0, pt[:, 0:T], ALU.add, ALU.mult, r=[ths[br][1], pk], w=[mk])
                mts.append((mt, mk))
            self.tt(mts[0][0], mts[0][0], mts[1][0], ALU.add, r=[mts[0][1], mts[1][1]], w=[mts[0][1]])
            self.tt(mixT[:, t, 0:T], mts[0][0], mts[2][0], ALU.add, r=[mts[0][1], mts[2][1]], w=[('ovA', t)])
            if nxt is not None and t % 2 == 0 and t // 2 + 1 < len(nsrc):
                self.P.tag = 'A1'
                self.a1_comp(t // 2 + 1)
                self.P.tag = 'Eg'
        if nxt is not None:
            self.P.tag = 'A'
            self.a2([r_ for (_, r_) in nsrc])
            nxt['a2_done'] = True
        self.P.tag = 'Eo'
        wo = [self.load_w('wo0'), self.load_w('wo1')]
        for qi, (q0, rows) in enumerate(QT):
            if qi == 3:
                slot = 3
                xt = self.ovB[:, 0:2, :].rearrange("p a b -> p (a b)")
                xk = ('ovB', 0)
                xkeys = [('ovB', 0), ('ovB', 1)]
            else:
                slot = self.x_i % 3
                self.x_i += 1
                xt = self.xr[slot]
                xk = ('x', slot)
                xkeys = [xk]
            self.dma_in(xt[0:rows, :], xsrc[qi][0], w=xkeys, dsem='x%d' % slot)
            for half in range(2):
                wt, wk = wo[half]
                pt, pk = self.ring()
                for c in range(8):
                    self.mm(pt[0:rows, :], mixT[:, c, q0:q0 + rows], wt[:, c * 512:(c + 1) * 512], c == 0, c == 7,
                            r=[('ovA', c), wk], w=[pk])
                self.stt(xt[0:rows, half * 512:(half + 1) * 512], pt[0:rows, :], 0.5, xt[0:rows, half * 512:(half + 1) * 512],
                         ALU.mult, ALU.add, r=[pk] + xkeys, w=xkeys)
            sa = 2 * slot
            self.act(self.junk[0:rows, :], xt[0:rows, :], AF.Square, r=xkeys, w=['junk', ('stat', sa)], accum=self.stat[0:rows, sa:sa + 1])
            self.rstd(rows, sa, sa + 1, D, tmpc=8 + slot)
            self.stt(xt[0:rows, :], xt[0:rows, :], self.stat[0:rows, sa + 1:sa + 2], self.nfin[0:rows, :], ALU.mult, ALU.mult,
                     r=xkeys + [('stat', sa + 1), 'nfin'], w=xkeys)
            dst = self.yp[s, pos0 + q0:pos0 + q0 + rows, :] if kind == 'p' else self.ys
            self.dma_out(dst, xt[0:rows, :], r=xkeys, dsem='ox%d' % slot)

    def etile(self):
        i = self.e_i % 8
        self.e_i += 1
        return self.ovA[:, i, :], ('ovA', i)

    def attn_B(self, kind, b, QT, kt0):
        NQ = len(QT)
        acc = self.accT
        accS = self.accS
        nkt = kt0 + NQ if kind == 'p' else 33
        sc = 64 ** -0.5
        LA = 2

        def accap(qi, m, rows):
            a = qi * 2 + m
            o0 = (a // 3) * 512 + (a % 3) * 129
            return acc[0:rows, o0:o0 + 129], ('acc', a // 3)

        def accs(qi, m, rows):
            a = qi * 2 + m
            return accS[0:rows, a // 3, (a % 3) * 129:(a % 3) * 129 + 129]

        def post(h):
            rows = QT[0][1]
            akeys = [('acc', 0), ('acc', 1), ('acc', 2)]
            self.cp(accS[0:rows, :, :], acc[0:rows, :].rearrange("p (b c) -> p b c", b=3)[:, :, 0:387], r=akeys, w=['accS'])
            rr = self.stat[0:rows, 16:25]
            rk = ('stat', 16)
            sums = accS[0:rows, :, :].rearrange("p b (i c) -> p b i c", c=129)[:, :, :, 128]
            self.op('dve', lambda e, sums=sums, rr=rr: e.reciprocal(out=rr.rearrange("p (b i) -> p b i", b=3), in_=sums), r=['accS'], w=[rk])
            self.ts(self.stat[0:rows, 17:25:2], self.stat[0:rows, 17:25:2], self.lam[0:rows, :], ALU.mult, r=[rk, 'lam'], w=[rk])
            pslot = h % 2
            pdb = self.ovC[:, pslot, :]
            pk_ = ('ovC', pslot)
            tmp = self.ovC[:, 2, 0:128]
            tk = ('ovC', 2)
            for qi, (q0, rows) in enumerate(QT):
                a1 = accs(qi, 0, rows)
                a2 = accs(qi, 1, rows)
                self.ts(tmp[0:rows, :], a2[:, 0:128], self.stat[0:rows, 16 + 2 * qi + 1:16 + 2 * qi + 2], ALU.mult, r=['accS', rk], w=[tk])
                self.stt(pdb[0:rows, qi * 128:(qi + 1) * 128], a1[:, 0:128], self.stat[0:rows, 16 + 2 * qi:16 + 2 * qi + 1], tmp[0:rows, :],
                         ALU.mult, ALU.subtract, r=['accS', rk, tk], w=[pk_])
            rows = QT[0][1]
            sq = self.junk[0:rows, 0:NQ * 128]
            self.tt(sq, pdb[0:rows, 0:NQ * 128], pdb[0:rows, 0:NQ * 128], ALU.mult, r=[pk_], w=['junk'])
            ssb = 32 + 8 * (h % 2)
            self.op('dve', lambda e, sq=sq, ssb=ssb, rows=rows, NQ=NQ: e.tensor_reduce(
                out=self.stat[0:rows, ssb:ssb + NQ], in_=sq.rearrange("p (q d) -> p q d", q=NQ), axis=AX.X, op=ALU.add), r=['junk'], w=[('stat', ssb)])
            self.rstd(rows, ssb, ssb + 4, 128, ncols=NQ, tmpc=60)
            for qi, (q0, rows) in enumerate(QT):
                self.stt(self.o[0:rows, qi, 512 + h * 128:512 + (h + 1) * 128], pdb[0:rows, qi * 128:(qi + 1) * 128],
                         self.stat[0:rows, ssb + 4 + qi:ssb + 5 + qi], self.sublnc[0:rows, :], ALU.mult, ALU.mult,
                         r=[pk_, ('stat', ssb + 4), 'sublnc'], w=[('o', qi, 1)])

        pend = []

        def flush_one():
            h, j, es, qis, nk, started = pend.pop(0)
            for qi in qis:
                q0, rows = QT[qi]
                lastj = (kt0 + qi) if kind == 'p' else 32
                for m in range(2):
                    ap_, ak_ = accap(qi, m, rows)
                    st = ak_ not in started
                    started.add(ak_)
                    et, ek = es[m]
                    self.mm(ap_, et[0:nk, q0:q0 + rows], self.bva[0:nk, j, h, 0:129], st, j == lastj, r=[ek, ('bva', j)], w=[ak_])
            if j == nkt - 1:
                post(h)

        for h in range(4):
            started = set()
            for j in range(nkt):
                if kind == 'p' and j >= kt0:
                    jj = j - kt0
                    qlo = jj * 128
                    qis = list(range(jj, NQ))
                else:
                    jj = None
                    qlo = 0
                    qis = list(range(NQ))
                nk = 16 if (kind == 's' and j == 32) else 128
                qhi = QT[-1][0] + QT[-1][1]
                es = []
                sps = []
                for m in range(2):
                    pt, pk = self.ring()
                    self.mm(pt[0:nk, qlo:qhi], self.bkT[64 * m:64 * m + 64, h, j * 128:j * 128 + nk], self.qT[64 * m:64 * m + 64, 4 + h, qlo:qhi],
                            True, True, r=[('bkT', j), ('qT', 4 + h)], w=[pk])
                    sps.append((pt, pk))
                for m in range(2):
                    pt, pk = sps[m]
                    et, ek = self.etile()
                    self.act(et[0:nk, qlo:qhi], pt[0:nk, qlo:qhi], AF.Exp, r=[pk], w=[ek], scale=sc)
                    if jj is not None:
                        self.op('dve', lambda e, et=et, qlo=qlo: e.memset(et[64:128, qlo:qlo + 64], 0.0), r=[], w=[ek])
                    es.append((et, ek))
                pend.append((h, j, es, qis, nk, started))
                if len(pend) > LA:
                    flush_one()
        while pend:
            flush_one()

    def attn_A(self, kind, b, QT, kt0):
        NQ = len(QT)
        acc = self.accT
        accS = self.accS
        sc = 64 ** -0.5
        LA = 2

        def accap(qi, hh, rows):
            a = qi * 2 + hh
            o0 = (a // 4) * 512 + (a % 4) * 65
            return acc[0:rows, o0:o0 + 65], ('acc', a // 4)
        if kind == 'p':
            jlist = [j for j in range(kt0 - 4, kt0 + 4) if j >= 0]
        else:
            jlist = [0, 1, 2, 3, 4]

        def post(p):
            rows = QT[0][1]
            nb = (2 * NQ + 3) // 4
            akeys = [('acc', 0), ('acc', 1)][0:nb]
            self.cp(accS[0:rows, 0:nb, 0:260], acc[0:rows, 0:1024].rearrange("p (b c) -> p b c", b=2)[:, 0:nb, 0:260], r=akeys, w=['accS'])
            rr = self.stat[0:rows, 48:56]
            rk = ('stat', 48)
            v = accS[0:rows, 0:nb, 0:260].rearrange("p b (i c) -> p b i c", c=65)
            self.op('dve', lambda e, v=v, rr=rr, nb=nb: e.reciprocal(out=rr[:, 0:4 * nb].rearrange("p (b i) -> p b i", b=nb), in_=v[:, :, :, 64]), r=['accS'], w=[rk])
            for bk in range(nb):
                nq2 = min(2, NQ - 2 * bk)
                outv = self.o[0:rows, 2 * bk:2 * bk + nq2, p * 128:(p + 1) * 128].rearrange("p q (h d) -> p q h d", h=2)
                inv = v[:, bk, 0:2 * nq2, 0:64].rearrange("p (q h) d -> p q h d", h=2)
                rb = rr[:, 4 * bk:4 * bk + 2 * nq2].rearrange("p (q h) -> p q h", h=2).unsqueeze(3).to_broadcast([rows, nq2, 2, 64])
                self.tt(outv, inv, rb, ALU.mult, r=['accS', rk], w=[('o', 2 * bk + q, 0) for q in range(nq2)])

        pend = []

        def flush_one():
            p, j, es, qis, nk, slot, started = pend.pop(0)
            for qi in qis:
                q0, rows = QT[qi]
                lastj = (kt0 + qi) if kind == 'p' else 4
                for hh in range(2):
                    ap_, ak_ = accap(qi, hh, rows)
                    st = ak_ not in started
                    started.add(ak_)
                    et, ek = es[hh]
                    self.mm(ap_, et[0:nk, q0:q0 + rows], self.ava[0:nk, slot, 2 * p + hh, 0:65], st, j == lastj, r=[ek, ('ava', slot)], w=[ak_])
            if j == jlist[-1]:
                post(p)

        for p in range(4):
            started = set()
            for j in jlist:
                if kind == 'p':
                    i_lo = max(j, kt0)
                    i_hi = min(j + 4, kt0 + 3)
                    qis = list(range(i_lo - kt0, i_hi - kt0 + 1))
                    d_lo = i_lo - j
                    slot = j % 8
                    nk = 128
                else:
                    qis = [0]
                    d_lo = 4 - j
                    slot = j
                    nk = 128 if j < 4 else 16
                c0 = QT[qis[0]][0]
                c1 = QT[qis[-1]][0] + QT[qis[-1]][1]
                ncol = c1 - c0
                sps = []
                for hh in range(2):
                    pt, pk = self.ring()
                    self.mm(pt[0:nk, c0:c1], self.akT[64 * hh:64 * hh + 64, p, slot * 128:slot * 128 + nk], self.qT[64 * hh:64 * hh + 64, p, c0:c1],
                            True, True, r=[('akT', slot), ('qT', p)], w=[pk])
                    sps.append((pt, pk))
                es = []
                for hh in range(2):
                    pt, pk = sps[hh]
                    et, ek = self.etile()
                    self.act(et[0:nk, c0:c1], pt[0:nk, c0:c1], AF.Exp, r=[pk], w=[ek], scale=sc)
                    head = 2 * p + hh
                    self.tt(et[0:nk, c0:c1], et[0:nk, c0:c1], self.EBM[0:nk, head, d_lo * 128:d_lo * 128 + ncol], ALU.mult,
                            r=[ek, 'EBM'], w=[ek], eng='dve')
                    es.append((et, ek))
                pend.append((p, j, es, qis, nk, slot, started))
                if len(pend) > LA:
                    flush_one()
        while pend:
            flush_one()

    def attn_M(self, QT, T, ms):
        NQ = len(QT)
        acc = self.accT
        accS = self.accS
        sc = 128 ** -0.5
        LA = 2

        def accap(qi, rows):
            o0 = (qi // 2) * 512 + (qi % 2) * 129
            return acc[0:rows, o0:o0 + 129], ('acc', qi // 2)

        def post(h):
            rows = QT[0][1]
            nb = (NQ + 1) // 2
            vp = acc[0:rows, 0:1024].rearrange("p (b c) -> p b c", b=2)[:, 0:nb, 0:258].rearrange("p b (i c) -> p b i c", c=129)
            for bk in range(nb):
                nq2 = min(2, NQ - 2 * bk)
                self.op('act', lambda e, bk=bk, nq2=nq2: e.activation(
                    out=self.o[0:rows, 2 * bk:2 * bk + nq2, 1024 + h * 128:1024 + (h + 1) * 128], in_=vp[:, bk, 0:nq2, 0:128], func=AF.Copy),
                    r=[('acc', bk)], w=[('o', 2 * bk + q, 2) for q in range(nq2)])
                c0 = 64 + (2 * bk) * 4 + h
                self.op('act', lambda e, bk=bk, nq2=nq2, c0=c0: e.activation(
                    out=self.stat[0:rows, c0:c0 + 4 * (nq2 - 1) + 1:4], in_=vp[:, bk, 0:nq2, 128], func=AF.Copy),
                    r=[('acc', bk)], w=[('stat', 64)])
            if h == 3:
                rr = self.stat[0:rows, 64:64 + 4 * NQ]
                self.op('dve', lambda e: e.reciprocal(out=rr, in_=rr), r=[('stat', 64)], w=[('stat', 64)])
                ov = self.o[0:rows, 0:NQ, 1024:1536].rearrange("p q (h d) -> p q h d", h=4)
                rb = rr.rearrange("p (q h) -> p q h", h=4).unsqueeze(3).to_broadcast([rows, NQ, 4, 128])
                self.tt(ov, ov, rb, ALU.mult, r=[('o', q, 2) for q in range(NQ)] + [('stat', 64)], w=[('o', q, 2) for q in range(NQ)])

        pend = []

        def flush_one():
            h, jm, et, ek, started = pend.pop(0)
            for qi, (q0, rows) in enumerate(QT):
                ap_, ak_ = accap(qi, rows)
                st = ak_ not in started
                started.add(ak_)
                self.mm(ap_, et[:, q0:q0 + rows], self.mva[ms][:, jm, h, 0:129], st, jm == 1, r=[ek, ('mva', ms)], w=[ak_])
            if jm == 1:
                post(h)

        for h in range(4):
            started = set()
            for jm in range(2):
                pt, pk = self.ring()
                self.mm(pt[:, 0:T], self.mkT[ms][:, h, jm * 128:(jm + 1) * 128], self.qT[:, 8 + h, 0:T], True, True,
                        r=[('mkT', ms), ('qT', 8 + h)], w=[pk])
                et, ek = self.etile()
                self.act(et[:, 0:T], pt[:, 0:T], AF.Exp, r=[pk], w=[ek], scale=sc)
                pend.append((h, jm, et, ek, started))
                if len(pend) > LA:
                    flush_one()
        while pend:
            flush_one()

    def load_sample_caches(self):
        for t in range(2):
            st, sk = self.stg()
            self.dma_in(st, self.cmk[t * 128:(t + 1) * 128, :], w=[sk], dsem='l_' + sk[0] + str(sk[1]))
            self.k_to_mkT(st, sk, 0, t)
            st, sk = self.stg()
            self.dma_in(st, self.cmv[t * 128:(t + 1) * 128, :], w=[sk], dsem='l_' + sk[0] + str(sk[1]))
            self.cp(self.mva[0][:, t, :, 0:128], st.rearrange("p (h d) -> p h d", h=4), r=[sk], w=[('mva', 0)])
        for t in range(4):
            st, sk = self.stg()
            self.dma_in(st, self.cak[t * 128:(t + 1) * 128, :], w=[sk], dsem='l_' + sk[0] + str(sk[1]))
            self.tm_to_fm(st, sk, self.akT[:, :, t * 128:(t + 1) * 128], [('akT', t)], 128)
            st, sk = self.stg()
            self.dma_in(st, self.cav[t * 128:(t + 1) * 128, :], w=[sk], dsem='l_' + sk[0] + str(sk[1]))
            self.cp(self.ava[:, t, :, 0:64], st.rearrange("p (h d) -> p h d", h=8), r=[sk], w=[('ava', t)])
        for j in range(32):
            st, sk = self.stg()
            self.dma_in(st, self.cbk[j * 128:(j + 1) * 128, :], w=[sk], dsem='l_' + sk[0] + str(sk[1]))
            self.tm_to_fm(st, sk, self.bkT[:, :, j * 128:(j + 1) * 128], [('bkT', j)], 128)
            st, sk = self.stg()
            self.dma_in(st, self.cbv[j * 128:(j + 1) * 128, :], w=[sk], dsem='l_' + sk[0] + str(sk[1]))
            self.cp(self.bva[:, j, :, 0:128], st.rearrange("p (h d) -> p h d", h=4), r=[sk], w=[('bva', j)])

    def build(self, limit=None):
        self.declare()
        self.P.tag = 'pro'
        self.prologue_consts()
        self.P.tag = 'cast'
        early = ['wm0', 'wm1', 'aq', 'ak', 'mq', 'av', 'bv', 'bq', 'bk', 'az', 'bz', 'mz']
        self.late_casts = ['m%d' % t for t in range(8)] + ['wo0', 'wo1']
        if limit != 'consts':
            for name in early:
                self.cast_item(name)
        if limit not in ('consts', 'cast'):
            self.P.tag = 'mem'
            self.memory_kv()
        if limit in ('cast', 'mem'):
            for name in self.late_casts:
                self.cast_item(name)
            self.late_casts = []
        if limit in ('consts', 'cast', 'mem'):
            self.P.tag = 'pro'
            self.prologue_ebm()
        if limit not in ('consts', 'cast', 'mem'):
            cfgs = []
            if isinstance(limit, list):
                cfgs = [dict(kind='p', s=s_, b=b_) for (s_, b_) in limit]
            else:
                for s in range(2):
                    for b in range(NBLK):
                        if isinstance(limit, int) and len(cfgs) >= limit:
                            continue
                        cfgs.append(dict(kind='p', s=s, b=b))
                if limit is None or limit == 'sample':
                    cfgs.append(dict(kind='s'))
            for i, c in enumerate(cfgs):
                c['next'] = cfgs[i + 1] if i + 1 < len(cfgs) else None
            self.P.tag = 'A1'
            self.a1(self.xsrc_of(cfgs[0]))
            self.P.tag = 'pro'
            self.prologue_ebm()
            for c in cfgs:
                if c['kind'] == 's':
                    self.P.tag = 'sload'
                    self.load_sample_caches()
                self.block(c)
        self.P.op('sp', lambda e: None, extra_deps=list(self.out_dmas))
        self.P.emit()


_CACHE = {}


def _get_program():
    if 'nc' not in _CACHE:
        nc = bass.Bass("TRN2", target_bir_lowering=False)
        bld = Builder(nc)
        bld.build()
        _CACHE['nc'] = nc
    return _CACHE['nc']


def _rope_table():
    d = 64
    inv = (1.0 / (np.float32(10000.0) ** (np.arange(0, d, 2, dtype=np.float32) / np.float32(d)))).astype(np.float32)
    pos = np.arange(NPOS, dtype=np.float32)
    ang = (pos[:, None] * inv[None, :]).astype(np.float32)
    return np.concatenate([np.cos(ang), np.sin(ang)], axis=1).astype(np.float32)


def kernel(x_prompt, x_sample, cache_a_k, cache_a_v, cache_b_k, cache_b_v, cache_mem_k, cache_mem_v,
           mem_prompt, norm_in, w_in, rel_bias, lambda_q1, lambda_k1, lambda_q2, lambda_k2, subln,
           norm_mem, w_mem_kv, w_branch_a, w_branch_b, w_branch_m, w_out, norm_final):
    f = lambda a: np.ascontiguousarray(np.asarray(a, dtype=np.float32))
    nc = _get_program()
    ident = np.eye(128, dtype=np.float32)
    cs = _rope_table()
    lam4 = np.concatenate([f(lambda_q1)[0], f(lambda_k1)[0], f(lambda_q2)[0], f(lambda_k2)[0]]).astype(np.float32)
    shared = dict(norm_in=f(norm_in)[0], w_in=f(w_in)[0], rel_bias=f(rel_bias)[0], lam4=lam4, subln=f(subln)[0],
                  norm_mem=f(norm_mem)[0], w_mem=f(w_mem_kv)[0], w_ba=f(w_branch_a)[0], w_bb=f(w_branch_b)[0],
                  w_bm=f(w_branch_m)[0], w_out=f(w_out)[0], norm_final=f(norm_final), ident=ident, cs=cs,
                  jrev=np.ascontiguousarray(ident[::-1]))
    xp = f(x_prompt)
    xs = f(x_sample)
    in_maps = []
    for c in range(NCORES):
        m = dict(shared)
        m['xp'] = xp[2 * c:2 * c + 2]
        m['xs'] = xs[c]
        m['cak'] = f(cache_a_k)[0, c].reshape(512, 512)
        m['cav'] = f(cache_a_v)[0, c].reshape(512, 512)
        m['cbk'] = f(cache_b_k)[0, c].reshape(S, 512)
        m['cbv'] = f(cache_b_v)[0, c].reshape(S, 512)
        m['cmk'] = f(cache_mem_k)[0, c].reshape(256, 512)
        m['cmv'] = f(cache_mem_v)[0, c].reshape(256, 512)
        m['memp'] = f(mem_prompt)[2 * c:2 * c + 2]
        in_maps.append(m)
    res = run_bass_kernel_spmd(nc, in_maps, core_ids=list(range(NCORES)))
    R = res.results
    cat = lambda k: np.concatenate([np.asarray(R[c][k], dtype=np.float32) for c in range(NCORES)], axis=0)
    stk = lambda k: np.stack([np.asarray(R[c][k], dtype=np.float32) for c in range(NCORES)], axis=0)
    y_prompt = cat('yp')
    y_sample = stk('ys')
    nak_p = cat('nak_p').reshape(1, 16, 512, 8, 64)
    nav_p = cat('nav_p').reshape(1, 16, 512, 8, 64)
    nbk_p = cat('nbk_p').reshape(1, 16, S, 8, 64)
    nbv_p = cat('nbv_p').reshape(1, 16, S, 4, 128)
    nmk_p = cat('nmk_p').reshape(1, 16, 256, 4, 128)
    nmv_p = cat('nmv_p').reshape(1, 16, 256, 4, 128)
    nak_s = stk('nak_s').reshape(1, 8, 16, 8, 64)
    nav_s = stk('nav_s').reshape(1, 8, 16, 8, 64)
    nbk_s = stk('nbk_s').reshape(1, 8, 16, 8, 64)
    nbv_s = stk('nbv_s').reshape(1, 8, 16, 4, 128)
    return (y_prompt, y_sample, nak_p, nav_p, nbk_p, nbv_p, nmk_p, nmv_p, nak_s, nav_s, nbk_s, nbv_s)
```

```python
import math
import numpy as np
import concourse.bass as bass
import concourse.mybir as mybir
from concourse.bass_utils import run_bass_kernel_spmd
from concourse.alu_op_type import AluOpType as ALU

AF = mybir.ActivationFunctionType
AX = mybir.AxisListType
F32 = mybir.dt.float32
BF16 = mybir.dt.bfloat16

NCORES = 8
S = 4096
D = 1024
TB = 512
NBLK = S // TB
EPS = 1e-6
LAM_INIT = 0.8 - 0.6 * math.exp(0.0)
ENGS = ('pe', 'act', 'dve', 'pool', 'sp')
COLS = dict(aq=0, ak=512, av=1024, az=1536, bq=2048, bk=2560, bv=3072, bz=3584, mq=4096, mz=4608)
GOFF = 5120
NPOS = S + 128


class Prog:
    def __init__(self, nc):
        self.nc = nc
        self.ins = []
        self.last_w = {}
        self.readers = {}

    def op(self, eng, fn, r=(), w=(), dsem=None, extra_deps=()):
        idx = len(self.ins)
        deps = set(extra_deps)
        for k in r:
            j = self.last_w.get(k)
            if j is not None:
                deps.add(j)
            if isinstance(k, tuple) and k[0] in ('ring', 'acc'):
                for j in self.readers.get(k, ()):
                    if self.ins[j]['eng'] != eng:
                        deps.add(j)
        for k in w:
            j = self.last_w.get(k)
            if j is not None:
                deps.add(j)
            for j in self.readers.get(k, ()):
                deps.add(j)
        for k in w:
            self.last_w[k] = idx
            self.readers[k] = []
        ws = set(w)
        for k in r:
            if k not in ws:
                lst = self.readers.setdefault(k, [])
                if dsem is None:
                    lst[:] = [j for j in lst if not (self.ins[j]['eng'] == eng and self.ins[j]['dsem'] is None)]
                lst.append(idx)
        deps.discard(idx)
        self.ins.append(dict(eng=eng, fn=fn, deps=deps, dsem=dsem, tag=getattr(self, 'tag', '')))
        return idx

    def emit(self):
        nc = self.nc
        ins = self.ins
        n = len(ins)
        is_target = [False] * n
        for i in ins:
            for j in i['deps']:
                is_target[j] = True
        cnt = {e: 0 for e in ENGS}
        comp = [None] * n
        dcount = {}
        for idx, i in enumerate(ins):
            if i['dsem'] is not None:
                k = i['dsem']
                dcount[k] = dcount.get(k, 0) + 1
                comp[idx] = (('d', k), 16 * dcount[k])
            elif is_target[idx]:
                cnt[i['eng']] += 1
                comp[idx] = (('e', i['eng']), cnt[i['eng']])
        for idx, i in enumerate(ins):
            if i['dsem'] is not None and str(i['dsem']).startswith('all:'):
                comp[idx] = (('d', i['dsem']), 16 * dcount[i['dsem']])
        sems = {}

        def getsem(key):
            if key not in sems:
                sems[key] = nc.alloc_semaphore('s%d' % len(sems))
            return sems[key]

        for e in ENGS:
            getsem(('e', e))
        streams = {e: [idx for idx in range(n) if ins[idx]['eng'] == e] for e in ENGS}
        self.nwaits = 0

        waits = [None] * n
        for e in ENGS:
            waited = {}
            for idx in streams[e]:
                i = ins[idx]
                need = {}
                for j in i['deps']:
                    sk, val = comp[j]
                    if sk == ('e', 'pe') and e == 'pe':
                        continue
                    if need.get(sk, 0) < val:
                        need[sk] = val
                wl = []
                for sk, val in need.items():
                    if waited.get(sk, 0) < val:
                        wl.append((sk, val))
                        waited[sk] = val
                        self.nwaits += 1
                waits[idx] = wl
        semv = {}
        pos = {e: 0 for e in ENGS}
        progress = True
        while progress:
            progress = False
            for e in ENGS:
                st = streams[e]
                while pos[e] < len(st):
                    idx = st[pos[e]]
                    if all(semv.get(sk, 0) >= val for sk, val in waits[idx]):
                        if comp[idx] is not None:
                            sk = comp[idx][0]
                            semv[sk] = semv.get(sk, 0) + (16 if sk[0] == 'd' else 1)
                        pos[e] += 1
                        progress = True
                    else:
                        break
        for e in ENGS:
            if pos[e] < len(streams[e]):
                idx = streams[e][pos[e]]
                raise RuntimeError("DEADLOCK: engine %s stuck at op %d waits %s have %s" % (
                    e, idx, waits[idx], [(sk, semv.get(sk, 0)) for sk, _ in waits[idx]]))

        def run(e, eng):
            for idx in streams[e]:
                i = ins[idx]
                for sk, val in waits[idx]:
                    eng.wait_ge(getsem(sk), val)
                r = i['fn'](eng)
                if r is not None:
                    i['name'] = r.ins.name
                if comp[idx] is not None:
                    sk, val = comp[idx]
                    assert r is not None, "target op emitted nothing"
                    r.then_inc(getsem(sk), 16 if sk[0] == 'd' else 1)

        with nc.Block() as block:
            @block.tensor
            def _(eng):
                run('pe', eng)

            @block.scalar
            def _(eng):
                run('act', eng)

            @block.vector
            def _(eng):
                run('dve', eng)

            @block.gpsimd
            def _(eng):
                run('pool', eng)

            @block.sync
            def _(eng):
                run('sp', eng)
        self.nsems = len(sems)


class Builder:
    def __init__(self, nc):
        self.nc = nc
        self.P = Prog(nc)
        self.out_dmas = []
        self.ring_i = 0
        self.w_i = 0
        self.x_i = 0
        self.stg_i = 0
        self.e_i = 0
        self.cs_i = 0
        self.dsem_n = 0

    def op(self, eng, fn, r=(), w=(), dsem=None):
        return self.P.op(eng, fn, r=r, w=w, dsem=dsem)

    def dma_in(self, out, in_, w, r=(), dsem=None, eng='sp', nc_ok=False):
        if dsem is None:
            dsem = 'all:p1'
        if nc_ok:
            return self.op(eng, lambda e: e.dma_start(out=out, in_=in_, allow_slow_non_contiguous=True), r=r, w=w, dsem=dsem)
        return self.op(eng, lambda e: e.dma_start(out=out, in_=in_), r=r, w=w, dsem=dsem)

    def dma_out(self, out, in_, r, dsem, eng='pool'):
        i = self.op(eng, lambda e: e.dma_start(out=out, in_=in_), r=r, dsem=dsem)
        self.out_dmas.append(i)
        return i

    def mm(self, out, lhsT, rhs, start, stop, r, w):
        return self.op('pe', lambda e: e.matmul(out, lhsT=lhsT, rhs=rhs, start=start, stop=stop, skip_group_check=True), r=r, w=w)

    def tr(self, out, in_, ident, r, w):
        return self.op('pe', lambda e: e.transpose(out=out, in_=in_, identity=ident), r=r, w=w)

    def act(self, out, in_, func, r, w, scale=None, bias=None, accum=None):
        kw = {}
        if scale is not None:
            kw['scale'] = scale
        if bias is not None:
            kw['bias'] = bias
        if accum is not None:
            kw['accum_out'] = accum
        return self.op('act', lambda e: e.activation(out=out, in_=in_, func=func, **kw), r=r, w=w)

    def tt(self, out, in0, in1, op, r, w, eng='dve'):
        return self.op(eng, lambda e: e.tensor_tensor(out=out, in0=in0, in1=in1, op=op), r=r, w=w)

    def ts(self, out, in0, s1, op0, r, w, s2=None, op1=None, eng='dve'):
        if op1 is None:
            return self.op(eng, lambda e: e.tensor_scalar(out=out, in0=in0, scalar1=s1, scalar2=None, op0=op0), r=r, w=w)
        return self.op(eng, lambda e: e.tensor_scalar(out=out, in0=in0, scalar1=s1, scalar2=s2, op0=op0, op1=op1), r=r, w=w)

    def stt(self, out, in0, scalar, in1, op0, op1, r, w):
        return self.op('dve', lambda e: e.scalar_tensor_tensor(out=out, in0=in0, scalar=scalar, in1=in1, op0=op0, op1=op1), r=r, w=w)

    def cp(self, out, in_, r, w, eng='dve'):
        if eng == 'act':
            return self.op('act', lambda e: e.activation(out=out, in_=in_, func=AF.Copy), r=r, w=w)
        return self.op(eng, lambda e: e.tensor_copy(out=out, in_=in_), r=r, w=w)

    def ring(self):
        b = self.ring_i % 5
        self.ring_i += 1
        return self.ringT[:, b * 512:(b + 1) * 512], ('ring', b)

    def declare(self):
        nc = self.nc
        di = lambda n, s: nc.dram_tensor(n, list(s), F32, kind="ExternalInput").ap()
        do = lambda n, s: nc.dram_tensor(n, list(s), F32, kind="ExternalOutput").ap()
        self.xp = di('xp', (2, S, D))
        self.xs = di('xs', (16, D))
        self.cak = di('cak', (512, 512))
        self.cav = di('cav', (512, 512))
        self.cbk = di('cbk', (S, 512))
        self.cbv = di('cbv', (S, 512))
        self.cmk = di('cmk', (256, 512))
        self.cmv = di('cmv', (256, 512))
        self.memp = di('memp', (2, 256, D))
        self.norm_in = di('norm_in', (D,))
        self.w_in = di('w_in', (D, 8192))
        self.rel_bias = di('rel_bias', (8, 257))
        self.lam4 = di('lam4', (256,))
        self.subln = di('subln', (128,))
        self.norm_mem = di('norm_mem', (D,))
        self.w_mem = di('w_mem', (D, 1024))
        self.w_br = [di('w_ba', (512, D)), di('w_bb', (512, D)), di('w_bm', (512, D))]
        self.w_out = di('w_out', (D, D))
        self.norm_final = di('norm_final', (D,))
        self.ident_d = di('ident', (128, 128))
        self.jrev_d = di('jrev', (128, 128))
        self.cs_d = di('cs', (NPOS, 64))
        self.yp = do('yp', (2, S, D))
        self.ys = do('ys', (16, D))
        self.nak_p = do('nak_p', (2, 512, 512))
        self.nav_p = do('nav_p', (2, 512, 512))
        self.nbk_p = do('nbk_p', (2, S, 512))
        self.nbv_p = do('nbv_p', (2, S, 512))
        self.nmk_p = do('nmk_p', (2, 256, 512))
        self.nmv_p = do('nmv_p', (2, 256, 512))
        self.nak_s = do('nak_s', (16, 512))
        self.nav_s = do('nav_s', (16, 512))
        self.nbk_s = do('nbk_s', (16, 512))
        self.nbv_s = do('nbv_s', (16, 512))
        sc = lambda n, L: nc.dram_tensor(n, [128, L], BF16, kind="Internal").ap()
        self.sc_win = {k: sc('sc_' + k, 4096) for k in COLS}
        self.sc_m = [sc('sc_m%d' % t, 4608) for t in range(8)]
        self.sc_wo = [sc('sc_wo%d' % h, 4096) for h in range(2)]
        self.sc_wm = [sc('sc_wm%d' % h, 4096) for h in range(2)]
        self.ext_d = nc.dram_tensor('ext', [8, 768], BF16, kind="Internal").ap()

        sb = lambda n, s, dt=F32: nc.alloc_sbuf_tensor(n, list(s), dt)
        self.ident_f = sb('ident_f', (128, 128))
        self.ident_b = sb('ident_b', (128, 128), BF16)
        self.jrev_b = sb('jrev_b', (128, 128), BF16)
        self.nin_b = sb('nin_b', (128, D))
        self.nfin = sb('nfin', (128, D))
        self.sublnc = sb('sublnc', (128, 128))
        self.l4 = sb('l4', (128, 256))
        self.lamt = sb('lamt', (128, 8))
        self.mhalf = sb('mhalf', (128, 16))
        self.stat = sb('stat', (128, 96))
        self.junk = sb('junk', (128, D), BF16)
        self.xr = [sb('xr%d' % i, (128, D)) for i in range(3)]
        self.hT = sb('hT', (128, 8, 512), BF16)
        self.wr = [sb('wr%d' % i, (128, 4608), BF16) for i in range(2)]
        self.bkT = sb('bkT', (128, 4, S + 128), BF16)
        self.bva = sb('bva', (128, 33, 4, 130), BF16)
        self.akT = sb('akT', (128, 4, 1024), BF16)
        self.ava = sb('ava', (128, 8, 8, 66), BF16)
        self.qT = sb('qT', (128, 12, 512), BF16)
        self.EBM = sb('EBM', (128, 8, 640), BF16)
        self.o = sb('o', (128, 4, 1536), BF16)
        self.mkT = [sb('mkT%d' % i, (128, 4, 256), BF16) for i in range(2)]
        self.mva = [sb('mva%d' % i, (128, 2, 4, 130), BF16) for i in range(2)]
        self.cst = [sb('cst%d' % i, (128, 4, 64)) for i in range(2)]
        self.ovA = sb('ovA', (128, 8, 512), BF16)
        self.ovB = sb('ovB', (128, 5, 512))
        self.ovC = sb('ovC', (128, 3, 512))
        self.accS = sb('accS', (128, 3, 387))
        self.ringT = nc.alloc_psum_tensor('ringT', [128, 5 * 512], F32)
        self.accT = nc.alloc_psum_tensor('accT', [128, 3 * 512], F32)

    def prologue_consts(self):
        nc = self.nc
        self.op('dve', lambda e: e.memset(self.accT[:, :], 1.0), w=[('acc', 0), ('acc', 1), ('acc', 2)])
        self.dma_in(self.ident_f[:], self.ident_d, w=['ident_f'])
        self.dma_in(self.ovB[:, 4, 0:128], self.jrev_d, w=[('ovB', 4)])
        self.dma_in(self.ovC[0:8, 0:2, :].rearrange("p a b -> p (a b)")[:, 0:256], self.rel_bias[:, 1:257], w=[('ovC', 0), ('ovC', 1)])
        self.cp(self.ident_b[:], self.ident_f[:], r=['ident_f'], w=['ident_b'])
        self.cp(self.jrev_b[:], self.ovB[:, 4, 0:128], r=[('ovB', 4)], w=['jrev_b'])
        self.dma_in(self.nin_b[:], self.norm_mem.partition_broadcast(128), w=['nin_b'], dsem='ninb')
        self.dma_in(self.nfin[:], self.norm_final.partition_broadcast(128), w=['nfin'])
        self.dma_in(self.sublnc[:], self.subln.partition_broadcast(128), w=['sublnc'])
        self.ts(self.sublnc[:], self.sublnc[:], 1.0 - LAM_INIT, ALU.mult, r=['sublnc'], w=['sublnc'])
        self.dma_in(self.l4[:], self.lam4.partition_broadcast(128), w=['l4'])
        self.op('dve', lambda e: e.memset(self.mhalf[:], -0.5), w=['mhalf'])
        l4 = self.l4
        lt = self.lamt
        self.tt(l4[:, 0:64], l4[:, 0:64], l4[:, 64:128], ALU.mult, r=['l4'], w=['l4'])
        self.tt(l4[:, 128:192], l4[:, 128:192], l4[:, 192:256], ALU.mult, r=['l4'], w=['l4'])
        self.op('dve', lambda e: e.tensor_reduce(out=lt[:, 0:2], in_=l4[:].rearrange("p (a b c) -> p a b c", a=2, b=2)[:, :, 0, :],
                                                 axis=AX.X, op=ALU.add), r=['l4'], w=['lamt'])
        self.act(lt[:, 2:4], lt[:, 0:2], AF.Exp, r=['lamt'], w=['lamt'])
        self.tt(lt[:, 4:5], lt[:, 2:3], lt[:, 3:4], ALU.subtract, r=['lamt'], w=['lamt'])
        self.ts(lt[:, 5:6], lt[:, 4:5], LAM_INIT, ALU.add, r=['lamt'], w=['lam'])
        self.lam = lt[:, 5:6]
        self.op('pool', lambda e: e.memset(self.bva[:, :, :, 128:130], 1.0), w=[('bva', j) for j in range(33)])
        self.op('pool', lambda e: e.memset(self.ava[:, :, :, 64:66], 1.0), w=[('ava', j) for j in range(8)])
        for i in range(2):
            self.op('pool', lambda e, i=i: e.memset(self.mva[i][:, :, :, 128:130], 1.0), w=[('mva', i)])

    def prologue_ebm(self):
        ex = self.ovC[:, 0:2, :].rearrange("p a b -> p (a b)")
        self.cp(ex[0:8, 256:768], ex[0:8, 255:256].to_broadcast([8, 512]), r=[('ovC', 0), ('ovC', 1)], w=[('ovC', 0), ('ovC', 1)])
        exb = self.junk
        self.act(exb[0:8, 0:768], ex[0:8, 0:768], AF.Exp, r=[('ovC', 0), ('ovC', 1)], w=['junk'])
        self.dma_in(self.ext_d, exb[0:8, 0:768], r=['junk'], w=['ext_d'], dsem='extst')
        qTf = self.qT[:, :, :].rearrange("p a b -> p (a b)")
        stgs = []
        for h in range(8):
            src = bass.AP(self.ext_d.tensor, h * 768, [[1, 128], [1, 640]])
            tmp = qTf[:, h * 640:(h + 1) * 640]
            keys = [('qT', c) for c in range((h * 640) // 512, ((h + 1) * 640 - 1) // 512 + 1)]
            self.dma_in(tmp, src, r=['ext_d'], w=[('ebs', h)], dsem='all:ebm')
            stgs.append((tmp, keys + [('ebs', h)]))
        self.ebm_stgs = stgs

    def prologue_ebm_b(self):
        stgs = self.ebm_stgs
        for h in range(8):
            tmp, keys = stgs[h]
            for (c0, c1) in ((0, 512), (512, 640)):
                pt, pk = self.ring()
                self.mm(pt[:, 0:c1 - c0], self.jrev_b[:], tmp[:, c0:c1], True, True, r=keys + ['jrev_b'], w=[pk])
                self.cp(self.EBM[:, h, c0:c1], pt[:, 0:c1 - c0], r=[pk], w=['EBM'], eng=('act' if h % 2 else 'dve'))
        self.op('dve', lambda e: e.memset(self.EBM[64:128, :, 0:64], 0.0), r=[], w=['EBM'])
        self.op('dve', lambda e: e.memset(self.EBM[0:64, :, 4 * 128 + 64:4 * 128 + 128], 0.0), r=[], w=['EBM'])

    def cast_dma(self, dst, src, key):
        k = getattr(self, 'cast_n', 0)
        self.cast_n = k + 1
        hist = getattr(self, 'cast_hist', [])
        deps = [hist[k - 4]] if k >= 4 else []
        i = self.P.op('pool', lambda e: e.dma_start(out=dst, in_=src), w=[key], dsem='c%d' % (k % 4), extra_deps=deps)
        hist.append(i)
        self.cast_hist = hist

    def cast_item(self, name):
        w_in = self.w_in
        if name in COLS:
            c0 = COLS[name]
            dst = self.sc_win[name].rearrange("p (c f) -> p c f", c=8)
            src = w_in[:, c0:c0 + 512].rearrange("(c p) f -> p c f", p=128)
            self.cast_dma(dst, src, ('sc', name, 0))
        elif name[0] == 'm':
            t = int(name[1:])
            for b in range(3):
                dst = self.sc_m[t][:, 0:3072].rearrange("p (c b f) -> p c b f", c=8, b=3)[:, :, b, :]
                c0 = GOFF + b * 1024 + t * 128
                src = w_in[:, c0:c0 + 128].rearrange("(c p) f -> p c f", p=128)
                self.cast_dma(dst, src, ('sc', name, b))
            for b in range(3):
                dst = self.sc_m[t][:, 3072:4608].rearrange("p (b c f) -> p b c f", b=3, c=4)[:, b, :, :]
                src = self.w_br[b][:, t * 128:(t + 1) * 128].rearrange("(c p) f -> p c f", p=128)
                self.cast_dma(dst, src, ('sc', name, 3 + b))
        elif name[0:2] == 'wo':
            h = int(name[2:])
            dst = self.sc_wo[h].rearrange("p (c f) -> p c f", c=8)
            src = self.w_out[:, h * 512:(h + 1) * 512].rearrange("(c p) f -> p c f", p=128)
            self.cast_dma(dst, src, ('sc', name, 0))
        elif name[0:2] == 'wm':
            h = int(name[2:])
            dst = self.sc_wm[h].rearrange("p (c f) -> p c f", c=8)
            src = self.w_mem[:, h * 512:(h + 1) * 512].rearrange("(c p) f -> p c f", p=128)
            self.cast_dma(dst, src, ('sc', name, 0))

    def sc_ap(self, name):
        if name in COLS:
            return self.sc_win[name], 4096
        if name[0] == 'm' and name not in COLS:
            return self.sc_m[int(name[1:])], 4608
        if name[0:2] == 'wo':
            return self.sc_wo[int(name[2:])], 4096
        return self.sc_wm[int(name[2:])], 4096

    def load_w(self, name):
        slot = self.w_i % 2
        self.w_i += 1
        src, L = self.sc_ap(name)
        t = self.wr[slot]
        self.dma_in(t[:, 0:L], src, r=[('sc', name, b) for b in range(6)], w=[('w', slot)], dsem='w%d' % slot)
        return t, ('w', slot)

    def rstd(self, rows, a, b, n, ncols=1, tmpc=60):
        st = self.stat
        I32 = mybir.dt.int32
        x = st[0:rows, a:a + ncols]
        y = st[0:rows, b:b + ncols]
        t = st[0:rows, tmpc:tmpc + ncols]
        ka, kb, kt = ('stat', a), ('stat', b), ('stat', tmpc)
        self.ts(x, x, 1.0 / n, ALU.mult, s2=EPS, op1=ALU.add, r=[ka], w=[ka])
        self.ts(y.bitcast(I32), x.bitcast(I32), -0.5, ALU.mult, s2=float(0x5f3759df), op1=ALU.add, r=[ka], w=[kb])
        for it in range(3):
            if ncols == 1:
                self.stt(t, x, y, y, ALU.mult, ALU.mult, r=[ka, kb], w=[kt])
            else:
                self.tt(t, x, y, ALU.mult, r=[ka, kb], w=[kt])
                self.tt(t, t, y, ALU.mult, r=[kt, kb], w=[kt])
            self.ts(t, t, -0.5, ALU.mult, s2=1.5, op1=ALU.add, r=[kt], w=[kt])
            self.tt(y, y, t, ALU.mult, r=[kb, kt], w=[kb])

    def a1_load(self, src_tiles, i):
        src, rows = src_tiles[i]
        slot = self.x_i % 3
        self.x_i += 1
        xt = self.xr[slot]
        self.dma_in(xt[0:rows, :], src, w=[('x', slot)], dsem='x%d' % slot)
        self.a1_slots[i] = (slot, rows)

    def a1_comp(self, i):
        slot, rows = self.a1_slots[i]
        xt = self.xr[slot]
        xk = ('x', slot)
        sa = 2 * slot
        self.act(self.junk[0:rows, :], xt[0:rows, :], AF.Square, r=[xk], w=['junk', ('stat', sa)], accum=self.stat[0:rows, sa:sa + 1])
        self.rstd(rows, sa, sa + 1, D, tmpc=8 + slot)
        self.stt(self.o[0:rows, i, 0:1024], xt[0:rows, :], self.stat[0:rows, sa + 1:sa + 2], self.nin_b[0:rows, :], ALU.mult, ALU.mult,
                 r=[xk, ('stat', sa + 1), 'nin_b'], w=[('o', i, 0), ('o', i, 1)])

    def a1(self, src_tiles):
        self.a1_slots = {}
        for i in range(len(src_tiles)):
            self.a1_load(src_tiles, i)
            self.a1_comp(i)

    def a2(self, tiles_rows, hcol0=0):
        for i, rows in enumerate(tiles_rows):
            col = hcol0 + i * 128
            pt, pk = self.ring()
            ptb = pt.bitcast(BF16)
            for c in range(8):
                self.tr(ptb[:, c * 128:c * 128 + rows], self.o[0:rows, i, c * 128:(c + 1) * 128], self.ident_b[0:rows, 0:rows],
                        r=[('o', i, 0), ('o', i, 1), 'ident_b'], w=[pk])
            self.cp(self.hT[:, :, col:col + rows], ptb[:, :].rearrange("p (a b) -> p a b", a=8)[:, :, 0:rows], r=[pk], w=[('hT', col // 128)],
                    eng=('act' if i % 2 else 'dve'))

    def proj_tm(self, wt, wk, qi, rows, hcol0=0):
        pt, pk = self.ring()
        col = hcol0 + qi * 128
        for c in range(8):
            self.mm(pt[0:rows, :], self.hT[:, c, col:col + rows], wt[:, c * 512:(c + 1) * 512], c == 0, c == 7,
                    r=[('hT', col // 128), wk], w=[pk])
        return pt, pk

    def proj_fm(self, wt, wk, f, T, F=512, sub=None):
        pt, pk = self.ring()
        for c in range(8):
            if sub is None:
                l = wt[:, c * F + f * 128:c * F + (f + 1) * 128]
            else:
                l = wt[:, c * F + sub * 128:c * F + (sub + 1) * 128]
            self.mm(pt[:, 0:T], l, self.hT[:, c, 0:T], c == 0, c == 7,
                    r=[('hT', q) for q in range((T + 127) // 128)] + (wk if isinstance(wk, list) else [wk]), w=[pk])
        return pt, pk

    def stg(self):
        i = self.stg_i % 5
        self.stg_i += 1
        return self.ovB[:, i, :], ('ovB', i)

    def memory_kv(self):
        tiles = []
        for s in range(2):
            for t in range(2):
                tiles.append((self.memp[s, t * 128:(t + 1) * 128, :], 128))
        self.a1(tiles)
        self.a2([128] * 4)
        self.dma_in(self.nin_b[:], self.norm_in.partition_broadcast(128), w=['nin_b'], dsem='ninb')
        for half in range(2):
            wt, wk = self.load_w('wm%d' % half)
            for qi in range(4):
                s, t = qi // 2, qi % 2
                pt, pk = self.proj_tm(wt, wk, qi, 128)
                st, sk = self.stg()
                self.cp(st, pt, r=[pk], w=[sk], eng='act')
                dst = (self.nmk_p if half == 0 else self.nmv_p)[s, t * 128:(t + 1) * 128, :]
                self.dma_out(dst, st, r=[sk], dsem='o_' + sk[0] + str(sk[1]), eng='sp')
                if half == 1:
                    self.cp(self.mva[s][:, t, :, 0:128], st.rearrange("p (h d) -> p h d", h=4), r=[sk], w=[('mva', s)])
                else:
                    self.k_to_mkT(st, sk, s, t)

    def k_to_mkT(self, st, sk, s, t, rows=128):
        pt, pk = self.ring()
        for h in range(4):
            self.tr(pt[:, h * 128:h * 128 + rows], st[0:rows, h * 128:(h + 1) * 128], self.ident_f[0:rows, 0:rows], r=[sk, 'ident_f'], w=[pk])
        self.cp(self.mkT[s][:, :, t * 128:t * 128 + rows], pt[:, :].rearrange("p (a b) -> p a b", a=4)[:, :, 0:rows], r=[pk], w=[('mkT', s)])

    def rope(self, pt, pk, dst, dk, cs, ck, rows):
        t1 = self.ovC[0:rows, 2, 0:256].rearrange("p (m f) -> p m f", m=8)
        t2 = self.ovC[0:rows, 2, 256:512].rearrange("p (m f) -> p m f", m=8)
        tk = ('ovC', 2)
        xin = pt[0:rows, :].rearrange("p (m t f) -> p m t f", m=8, t=2)
        x1, x2 = xin[:, :, 0, :], xin[:, :, 1, :]
        dv = dst[0:rows, :].rearrange("p (m t f) -> p m t f", m=8, t=2)
        cos = cs[0:rows, 0:32].unsqueeze(1).to_broadcast([rows, 8, 32])
        sin = cs[0:rows, 32:64].unsqueeze(1).to_broadcast([rows, 8, 32])
        self.tt(t1, x1, cos, ALU.mult, r=[pk, ck], w=[tk])
        self.tt(t2, x2, sin, ALU.mult, r=[pk, ck], w=[tk])
        self.tt(dv[:, :, 0, :], t1, t2, ALU.subtract, r=[tk], w=[dk])
        self.tt(t1, x1, sin, ALU.mult, r=[pk, ck, dk], w=[tk])
        self.tt(t2, x2, cos, ALU.mult, r=[pk, ck], w=[tk])
        self.tt(dv[:, :, 1, :], t1, t2, ALU.add, r=[tk], w=[dk])

    def tm_to_fm(self, st, sk, dst_ap, dkeys, rows):
        pt, pk = self.ring()
        for f in range(4):
            self.tr(pt[:, f * 128:f * 128 + rows], st[0:rows, f * 128:(f + 1) * 128], self.ident_f[0:rows, 0:rows], r=[sk, 'ident_f'], w=[pk])
        self.cp(dst_ap, pt[:, :].rearrange("p (a b) -> p a b", a=4)[:, :, 0:rows], r=[pk], w=dkeys)

    def xsrc_of(self, cfg):
        if cfg['kind'] == 'p':
            s, b = cfg['s'], cfg['b']
            return [(self.xp[s, b * 512 + i * 128:b * 512 + (i + 1) * 128, :], 128) for i in range(4)]
        return [(self.xs, 16)]

    def block(self, cfg):
        kind = cfg['kind']
        if kind == 'p':
            s, b = cfg['s'], cfg['b']
            T = 512
            QT = [(i * 128, 128) for i in range(4)]
            pos0 = b * 512
            xsrc = [(self.xp[s, pos0 + i * 128:pos0 + (i + 1) * 128, :], 128) for i in range(4)]
            memslot = s
            last = (b == NBLK - 1)
            kt0 = 4 * b
        else:
            T = 16
            QT = [(0, 16)]
            pos0 = S
            xsrc = [(self.xs, 16)]
            memslot = 0
            last = True
            kt0 = 32
            b = None
        NQ = len(QT)
        hk = [('hT', q) for q in range(NQ)]
        csi = self.cs_i % 2
        self.cs_i += 1
        cst = self.cst[csi]
        ck = ('cs', csi)
        if kind == 'p':
            self.dma_in(cst[:], self.cs_d[pos0:pos0 + 512, :].rearrange("(t p) f -> p t f", p=128), w=[ck], dsem='cs%d' % csi)
        else:
            self.dma_in(cst[0:16, 0, :], self.cs_d[pos0:pos0 + 16, :], w=[ck], dsem='cs%d' % csi)
        if not cfg.get('a2_done'):
            self.P.tag = 'A'
            self.a2([r_ for (_, r_) in QT])

        out_bk = (lambda qi: self.nbk_p[s, pos0 + qi * 128:pos0 + (qi + 1) * 128, :]) if kind == 'p' else (lambda qi: self.nbk_s)
        out_bv = (lambda qi: self.nbv_p[s, pos0 + qi * 128:pos0 + (qi + 1) * 128, :]) if kind == 'p' else (lambda qi: self.nbv_s)
        out_ak = (lambda qi: self.nak_p[s, qi * 128:(qi + 1) * 128, :]) if kind == 'p' else (lambda qi: self.nak_s)
        out_av = (lambda qi: self.nav_p[s, qi * 128:(qi + 1) * 128, :]) if kind == 'p' else (lambda qi: self.nav_s)
        qT = self.qT
        deferred = []
        for name in ('aq', 'ak', 'mq', 'av', 'bv', 'bq', 'bk'):
            self.P.tag = 'B_' + name
            wt, wk = self.load_w(name)
            if name in ('bq', 'bk'):
                for qi, (q0, rows) in enumerate(QT):
                    pt, pk = self.proj_tm(wt, wk, qi, rows)
                    st, sk = self.stg()
                    self.rope(pt, pk, st, sk, cst[:, qi, :], ck, rows)
                    while len(deferred) > 0:
                        deferred.pop(0)()
                    if name == 'bk':
                        self.dma_out(out_bk(qi), st[0:rows, :], r=[sk], dsem='o_' + sk[0] + str(sk[1]))
                        j = kt0 + qi
                        deferred.append(lambda st=st, sk=sk, j=j, rows=rows: self.tm_to_fm(st, sk, self.bkT[:, :, j * 128:j * 128 + rows], [('bkT', j)], rows))
                    else:
                        deferred.append(lambda st=st, sk=sk, q0=q0, rows=rows: self.tm_to_fm(st, sk, qT[:, 4:8, q0:q0 + rows], [('qT', 4 + f) for f in range(4)], rows))
            elif name == 'bv':
                for qi, (q0, rows) in enumerate(QT):
                    pt, pk = self.proj_tm(wt, wk, qi, rows)
                    st, sk = self.stg()
                    self.cp(st[0:rows, :], pt[0:rows, :], r=[pk], w=[sk], eng='act')
                    self.dma_out(out_bv(qi), st[0:rows, :], r=[sk], dsem='o_' + sk[0] + str(sk[1]))
                    j = kt0 + qi
                    self.cp(self.bva[0:rows, j, :, 0:128], st[0:rows, :].rearrange("p (h d) -> p h d", h=4), r=[sk], w=[('bva', j)], eng='act')
            elif name == 'av':
                for qi, (q0, rows) in enumerate(QT):
                    pt, pk = self.proj_tm(wt, wk, qi, rows)
                    slot = (kt0 + qi) % 8 if kind == 'p' else 4
                    self.cp(self.ava[0:rows, slot, :, 0:64], pt[0:rows, :].rearrange("p (h d) -> p h d", h=8), r=[pk], w=[('ava', slot)], eng='act')
                    if last:
                        st, sk = self.stg()
                        self.cp(st[0:rows, :], pt[0:rows, :], r=[pk], w=[sk], eng='act')
                        self.dma_out(out_av(qi), st[0:rows, :], r=[sk], dsem='o_' + sk[0] + str(sk[1]))
            elif name in ('aq', 'mq'):
                base = 0 if name == 'aq' else 8
                for f in range(4):
                    pt, pk = self.proj_fm(wt, wk, f, T)
                    self.cp(qT[:, base + f, 0:T], pt[:, 0:T], r=[pk], w=[('qT', base + f)], eng='act')
            elif name == 'ak':
                for f in range(4):
                    pt, pk = self.proj_fm(wt, wk, f, T)
                    if kind == 'p':
                        c0 = (kt0 % 8) * 128
                        keys = [('akT', (kt0 + q) % 8) for q in range(4)]
                    else:
                        c0 = 4 * 128
                        keys = [('akT', 4)]
                    self.cp(self.akT[:, f, c0:c0 + T], pt[:, 0:T], r=[pk], w=keys, eng='act')
                if last:
                    for qi, (q0, rows) in enumerate(QT):
                        pt, pk = self.proj_tm(wt, wk, qi, rows)
                        st, sk = self.stg()
                        self.cp(st[0:rows, :], pt[0:rows, :], r=[pk], w=[sk], eng='act')
                        self.dma_out(out_ak(qi), st[0:rows, :], r=[sk], dsem='o_' + sk[0] + str(sk[1]))

        if self.late_casts:
            self.P.tag = 'cast'
            for nm in self.late_casts:
                self.cast_item(nm)
            self.late_casts = []
        nxt = cfg.get('next')
        if nxt is not None:
            self.P.tag = 'A1'
            nsrc = self.xsrc_of(nxt)
            self.a1_slots = {}
            for i in range(min(3, len(nsrc))):
                self.a1_load(nsrc, i)
        self.P.tag = 'attA'
        self.attn_A(kind, b, QT, kt0)
        while len(deferred) > 0:
            deferred.pop(0)()
        self.P.tag = 'Ez'
        azbuf = self.ovB[:, :, :].rearrange("p a b -> p (a b)").bitcast(BF16)
        azk = [('ovB', i) for i in range(4)]
        src_az, L_az = self.sc_ap('az')
        self.dma_in(azbuf[:, 0:L_az], src_az, r=[('sc', 'az', b_) for b_ in range(6)], w=azk, dsem='azp')
        self.P.tag = 'attB'
        self.attn_B(kind, b, QT, kt0)
        self.P.tag = 'attM'
        self.attn_M(QT, T, memslot)
        self.P.tag = 'D'

        for qi, (q0, rows) in enumerate(QT):
            for g in range(3):
                pt, pk = self.ring()
                ptb = pt.bitcast(BF16)
                for cc in range(4):
                    c = g * 4 + cc
                    self.tr(ptb[:, cc * 128:cc * 128 + rows], self.o[0:rows, qi, c * 128:(c + 1) * 128], self.ident_b[0:rows, 0:rows],
                            r=[('o', qi, g), 'ident_b'], w=[pk])
                self.cp(qT[:, g * 4:(g + 1) * 4, q0:q0 + rows], ptb[:, 0:512].rearrange("p (a b) -> p a b", a=4)[:, :, 0:rows],
                        r=[pk], w=[('qT', g * 4 + cc) for cc in range(4)])

        self.P.tag = 'Ez'
        for g, name in enumerate(('az', 'bz', 'mz')):
            if name == 'az':
                wt, wk = azbuf, azk
            else:
                wt, wk = self.load_w(name)
            for f in range(4):
                c = g * 4 + f
                pt, pk = self.proj_fm(wt, wk, f, T)
                tz, tzk = self.ovC[:, 0, 0:T], ('ovC', 0)
                sz, szk = self.ovC[:, 1, 0:T], ('ovC', 1)
                self.act(tz, pt[:, 0:T], AF.Tanh, r=[pk], w=[tzk], scale=0.5)
                self.stt(sz, tz, 1.0, pt[:, 0:T], ALU.add, ALU.mult, r=[tzk, pk], w=[szk])
                self.stt(qT[:, c, 0:T], sz, 0.5, qT[:, c, 0:T], ALU.mult, ALU.mult, r=[szk, ('qT', c)], w=[('qT', c)])
            if nxt is not None and g == 0:
                self.P.tag = 'A1'
                self.a1_comp(0)
                self.P.tag = 'Ez'
        if nxt is not None and len(nsrc) > 3:
            self.P.tag = 'A1'
            self.a1_load(nsrc, 3)
        self.P.tag = 'Eg'
        mixT = self.ovA
        thv = self.ovB[:, 0:3, :].bitcast(BF16)
        for t in range(8):
            mt_, gk = self.load_w('m%d' % t)
            gt = mt_[:, 0:3072]
            wbt, wbk = mt_[:, 3072:4608], gk
            ths = []
            for br in range(3):
                pt, pk = self.proj_fm(gt, gk, None, T, F=384, sub=br)
                slot = (t % 2) * 3 + br
                th = thv[:, slot // 2, (slot % 2) * 512:(slot % 2) * 512 + T]
                thk = ('ovB', slot // 2)
                self.act(th, pt[:, 0:T], AF.Tanh, r=[pk], w=[thk], scale=0.5)
                ths.append((th, thk))
            mts = []
            for br in range(3):
                pt, pk = self.ring()
                for cc in range(4):
                    self.mm(pt[:, 0:T], wbt[:, (br * 4 + cc) * 128:(br * 4 + cc + 1) * 128], qT[:, br * 4 + cc, 0:T], cc == 0, cc == 3,
                            r=[wbk, ('qT', br * 4 + cc)], w=[pk])
                mt, mk = self.ovC[:, br, 0:T], ('ovC', br)
                self.stt(mt, ths[br][0], 1.0, pt[:, 0:T], ALU.add, ALU.mult, r=[ths[br][1], pk], w=[mk])
                mts.append((mt, mk))
            self.tt(mts[0][0], mts[0][0], mts[1][0], ALU.add, r=[mts[0][1], mts[1][1]], w=[mts[0][1]])
            self.tt(mixT[:, t, 0:T], mts[0][0], mts[2][0], ALU.add, r=[mts[0][1], mts[2][1]], w=[('ovA', t)])
            if nxt is not None and t % 2 == 0 and t // 2 + 1 < len(nsrc):
                self.P.tag = 'A1'
                self.a1_comp(t // 2 + 1)
                self.P.tag = 'Eg'
        if nxt is not None:
            self.P.tag = 'A'
            self.a2([r_ for (_, r_) in nsrc])
            nxt['a2_done'] = True
        self.P.tag = 'Eo'
        wo = [self.load_w('wo0'), self.load_w('wo1')]
        for qi, (q0, rows) in enumerate(QT):
            if qi == 3:
                slot = 3
                xt = self.ovB[:, 0:2, :].rearrange("p a b -> p (a b)")
                xk = ('ovB', 0)
                xkeys = [('ovB', 0), ('ovB', 1)]
            else:
                slot = self.x_i % 3
                self.x_i += 1
                xt = self.xr[slot]
                xk = ('x', slot)
                xkeys = [xk]
            self.dma_in(xt[0:rows, :], xsrc[qi][0], w=xkeys, dsem='x%d' % slot)
            for half in range(2):
                wt, wk = wo[half]
                pt, pk = self.ring()
                for c in range(8):
                    self.mm(pt[0:rows, :], mixT[:, c, q0:q0 + rows], wt[:, c * 512:(c + 1) * 512], c == 0, c == 7,
                            r=[('ovA', c), wk], w=[pk])
                self.stt(xt[0:rows, half * 512:(half + 1) * 512], pt[0:rows, :], 0.5, xt[0:rows, half * 512:(half + 1) * 512],
                         ALU.mult, ALU.add, r=[pk] + xkeys, w=xkeys)
            sa = 2 * slot
            self.act(self.junk[0:rows, :], xt[0:rows, :], AF.Square, r=xkeys, w=['junk', ('stat', sa)], accum=self.stat[0:rows, sa:sa + 1])
            self.rstd(rows, sa, sa + 1, D, tmpc=8 + slot)
            self.stt(xt[0:rows, :], xt[0:rows, :], self.stat[0:rows, sa + 1:sa + 2], self.nfin[0:rows, :], ALU.mult, ALU.mult,
                     r=xkeys + [('stat', sa + 1), 'nfin'], w=xkeys)
            dst = self.yp[s, pos0 + q0:pos0 + q0 + rows, :] if kind == 'p' else self.ys
            self.dma_out(dst, xt[0:rows, :], r=xkeys, dsem='ox%d' % slot)

    def etile(self):
        i = self.e_i % 8
        self.e_i += 1
        return self.ovA[:, i, :], ('ovA', i)

    def attn_B(self, kind, b, QT, kt0):
        NQ = len(QT)
        acc = self.accT
        accS = self.accS
        nkt = kt0 + NQ if kind == 'p' else 33
        sc = 64 ** -0.5
        LA = 2

        def accap(qi, m, rows):
            a = qi * 2 + m
            o0 = (a // 3) * 512 + (a % 3) * 129
            return acc[0:rows, o0:o0 + 129], ('acc', a // 3)

        def accs(qi, m, rows):
            a = qi * 2 + m
            return accS[0:rows, a // 3, (a % 3) * 129:(a % 3) * 129 + 129]

        def post(h):
            rows = QT[0][1]
            akeys = [('acc', 0), ('acc', 1), ('acc', 2)]
            self.cp(accS[0:rows, :, :], acc[0:rows, :].rearrange("p (b c) -> p b c", b=3)[:, :, 0:387], r=akeys, w=['accS'])
            rr = self.stat[0:rows, 16:25]
            rk = ('stat', 16)
            sums = accS[0:rows, :, :].rearrange("p b (i c) -> p b i c", c=129)[:, :, :, 128]
            self.op('dve', lambda e, sums=sums, rr=rr: e.reciprocal(out=rr.rearrange("p (b i) -> p b i", b=3), in_=sums), r=['accS'], w=[rk])
            self.ts(self.stat[0:rows, 17:25:2], self.stat[0:rows, 17:25:2], self.lam[0:rows, :], ALU.mult, r=[rk, 'lam'], w=[rk])
            pslot = h % 2
            pdb = self.ovC[:, pslot, :]
            pk_ = ('ovC', pslot)
            tmp = self.ovC[:, 2, 0:128]
            tk = ('ovC', 2)
            for qi, (q0, rows) in enumerate(QT):
                a1 = accs(qi, 0, rows)
                a2 = accs(qi, 1, rows)
                self.ts(tmp[0:rows, :], a2[:, 0:128], self.stat[0:rows, 16 + 2 * qi + 1:16 + 2 * qi + 2], ALU.mult, r=['accS', rk], w=[tk])
                self.stt(pdb[0:rows, qi * 128:(qi + 1) * 128], a1[:, 0:128], self.stat[0:rows, 16 + 2 * qi:16 + 2 * qi + 1], tmp[0:rows, :],
                         ALU.mult, ALU.subtract, r=['accS', rk, tk], w=[pk_])
            rows = QT[0][1]
            sq = self.junk[0:rows, 0:NQ * 128]
            self.tt(sq, pdb[0:rows, 0:NQ * 128], pdb[0:rows, 0:NQ * 128], ALU.mult, r=[pk_], w=['junk'])
            ssb = 32 + 8 * (h % 2)
            self.op('dve', lambda e, sq=sq, ssb=ssb, rows=rows, NQ=NQ: e.tensor_reduce(
                out=self.stat[0:rows, ssb:ssb + NQ], in_=sq.rearrange("p (q d) -> p q d", q=NQ), axis=AX.X, op=ALU.add), r=['junk'], w=[('stat', ssb)])
            self.rstd(rows, ssb, ssb + 4, 128, ncols=NQ, tmpc=60)
            for qi, (q0, rows) in enumerate(QT):
                self.stt(self.o[0:rows, qi, 512 + h * 128:512 + (h + 1) * 128], pdb[0:rows, qi * 128:(qi + 1) * 128],
                         self.stat[0:rows, ssb + 4 + qi:ssb + 5 + qi], self.sublnc[0:rows, :], ALU.mult, ALU.mult,
                         r=[pk_, ('stat', ssb + 4), 'sublnc'], w=[('o', qi, 1)])

        pend = []

        def flush_one():
            h, j, es, qis, nk, started = pend.pop(0)
            for qi in qis:
                q0, rows = QT[qi]
                lastj = (kt0 + qi) if kind == 'p' else 32
                for m in range(2):
                    ap_, ak_ = accap(qi, m, rows)
                    st = ak_ not in started
                    started.add(ak_)
                    et, ek = es[m]
                    self.mm(ap_, et[0:nk, q0:q0 + rows], self.bva[0:nk, j, h, 0:129], st, j == lastj, r=[ek, ('bva', j)], w=[ak_])
            if j == nkt - 1:
                post(h)

        for h in range(4):
            started = set()
            for j in range(nkt):
                if kind == 'p' and j >= kt0:
                    jj = j - kt0
                    qlo = jj * 128
                    qis = list(range(jj, NQ))
                else:
                    jj = None
                    qlo = 0
                    qis = list(range(NQ))
                nk = 16 if (kind == 's' and j == 32) else 128
                qhi = QT[-1][0] + QT[-1][1]
                es = []
                sps = []
                for m in range(2):
                    pt, pk = self.ring()
                    self.mm(pt[0:nk, qlo:qhi], self.bkT[64 * m:64 * m + 64, h, j * 128:j * 128 + nk], self.qT[64 * m:64 * m + 64, 4 + h, qlo:qhi],
                            True, True, r=[('bkT', j), ('qT', 4 + h)], w=[pk])
                    sps.append((pt, pk))
                for m in range(2):
                    pt, pk = sps[m]
                    et, ek = self.etile()
                    self.act(et[0:nk, qlo:qhi], pt[0:nk, qlo:qhi], AF.Exp, r=[pk], w=[ek], scale=sc)
                    if jj is not None:
                        self.op('dve', lambda e, et=et, qlo=qlo: e.memset(et[64:128, qlo:qlo + 64], 0.0), r=[], w=[ek])
                    es.append((et, ek))
                pend.append((h, j, es, qis, nk, started))
                if len(pend) > LA:
                    flush_one()
        while pend:
            flush_one()

    def attn_A(self, kind, b, QT, kt0):
        NQ = len(QT)
        acc = self.accT
        accS = self.accS
        sc = 64 ** -0.5
        LA = 2

        def accap(qi, hh, rows):
            a = qi * 2 + hh
            o0 = (a // 4) * 512 + (a % 4) * 65
            return acc[0:rows, o0:o0 + 65], ('acc', a // 4)
        if kind == 'p':
            jlist = [j for j in range(kt0 - 4, kt0 + 4) if j >= 0]
        else:
            jlist = [0, 1, 2, 3, 4]

        def post(p):
            rows = QT[0][1]
            nb = (2 * NQ + 3) // 4
            akeys = [('acc', 0), ('acc', 1)][0:nb]
            self.cp(accS[0:rows, 0:nb, 0:260], acc[0:rows, 0:1024].rearrange("p (b c) -> p b c", b=2)[:, 0:nb, 0:260], r=akeys, w=['accS'])
            rr = self.stat[0:rows, 48:56]
            rk = ('stat', 48)
            v = accS[0:rows, 0:nb, 0:260].rearrange("p b (i c) -> p b i c", c=65)
            self.op('dve', lambda e, v=v, rr=rr, nb=nb: e.reciprocal(out=rr[:, 0:4 * nb].rearrange("p (b i) -> p b i", b=nb), in_=v[:, :, :, 64]), r=['accS'], w=[rk])
            for bk in range(nb):
                nq2 = min(2, NQ - 2 * bk)
                outv = self.o[0:rows, 2 * bk:2 * bk + nq2, p * 128:(p + 1) * 128].rearrange("p q (h d) -> p q h d", h=2)
                inv = v[:, bk, 0:2 * nq2, 0:64].rearrange("p (q h) d -> p q h d", h=2)
                rb = rr[:, 4 * bk:4 * bk + 2 * nq2].rearrange("p (q h) -> p q h", h=2).unsqueeze(3).to_broadcast([rows, nq2, 2, 64])
                self.tt(outv, inv, rb, ALU.mult, r=['accS', rk], w=[('o', 2 * bk + q, 0) for q in range(nq2)])

        pend = []

        def flush_one():
            p, j, es, qis, nk, slot, started = pend.pop(0)
            for qi in qis:
                q0, rows = QT[qi]
                lastj = (kt0 + qi) if kind == 'p' else 4
                for hh in range(2):
                    ap_, ak_ = accap(qi, hh, rows)
                    st = ak_ not in started
                    started.add(ak_)
                    et, ek = es[hh]
                    self.mm(ap_, et[0:nk, q0:q0 + rows], self.ava[0:nk, slot, 2 * p + hh, 0:65], st, j == lastj, r=[ek, ('ava', slot)], w=[ak_])
            if j == jlist[-1]:
                post(p)

        for p in range(4):
            started = set()
            for j in jlist:
                if kind == 'p':
                    i_lo = max(j, kt0)
                    i_hi = min(j + 4, kt0 + 3)
                    qis = list(range(i_lo - kt0, i_hi - kt0 + 1))
                    d_lo = i_lo - j
                    slot = j % 8
                    nk = 128
                else:
                    qis = [0]
                    d_lo = 4 - j
                    slot = j
                    nk = 128 if j < 4 else 16
                c0 = QT[qis[0]][0]
                c1 = QT[qis[-1]][0] + QT[qis[-1]][1]
                ncol = c1 - c0
                sps = []
                for hh in range(2):
                    pt, pk = self.ring()
                    self.mm(pt[0:nk, c0:c1], self.akT[64 * hh:64 * hh + 64, p, slot * 128:slot * 128 + nk], self.qT[64 * hh:64 * hh + 64, p, c0:c1],
                            True, True, r=[('akT', slot), ('qT', p)], w=[pk])
                    sps.append((pt, pk))
                es = []
                for hh in range(2):
                    pt, pk = sps[hh]
                    et, ek = self.etile()
                    self.act(et[0:nk, c0:c1], pt[0:nk, c0:c1], AF.Exp, r=[pk], w=[ek], scale=sc)
                    head = 2 * p + hh
                    self.tt(et[0:nk, c0:c1], et[0:nk, c0:c1], self.EBM[0:nk, head, d_lo * 128:d_lo * 128 + ncol], ALU.mult,
                            r=[ek, 'EBM'], w=[ek], eng='dve')
                    es.append((et, ek))
                pend.append((p, j, es, qis, nk, slot, started))
                if len(pend) > LA:
                    flush_one()
        while pend:
            flush_one()

    def attn_M(self, QT, T, ms):
        NQ = len(QT)
        acc = self.accT
        accS = self.accS
        sc = 128 ** -0.5
        LA = 2

        def accap(qi, rows):
            o0 = (qi // 2) * 512 + (qi % 2) * 129
            return acc[0:rows, o0:o0 + 129], ('acc', qi // 2)

        def post(h):
            rows = QT[0][1]
            nb = (NQ + 1) // 2
            vp = acc[0:rows, 0:1024].rearrange("p (b c) -> p b c", b=2)[:, 0:nb, 0:258].rearrange("p b (i c) -> p b i c", c=129)
            for bk in range(nb):
                nq2 = min(2, NQ - 2 * bk)
                self.op('act', lambda e, bk=bk, nq2=nq2: e.activation(
                    out=self.o[0:rows, 2 * bk:2 * bk + nq2, 1024 + h * 128:1024 + (h + 1) * 128], in_=vp[:, bk, 0:nq2, 0:128], func=AF.Copy),
                    r=[('acc', bk)], w=[('o', 2 * bk + q, 2) for q in range(nq2)])
                c0 = 64 + (2 * bk) * 4 + h
                self.op('act', lambda e, bk=bk, nq2=nq2, c0=c0: e.activation(
                    out=self.stat[0:rows, c0:c0 + 4 * (nq2 - 1) + 1:4], in_=vp[:, bk, 0:nq2, 128], func=AF.Copy),
                    r=[('acc', bk)], w=[('stat', 64)])
            if h == 3:
                rr = self.stat[0:rows, 64:64 + 4 * NQ]
                self.op('dve', lambda e: e.reciprocal(out=rr, in_=rr), r=[('stat', 64)], w=[('stat', 64)])
                ov = self.o[0:rows, 0:NQ, 1024:1536].rearrange("p q (h d) -> p q h d", h=4)
                rb = rr.rearrange("p (q h) -> p q h", h=4).unsqueeze(3).to_broadcast([rows, NQ, 4, 128])
                self.tt(ov, ov, rb, ALU.mult, r=[('o', q, 2) for q in range(NQ)] + [('stat', 64)], w=[('o', q, 2) for q in range(NQ)])

        pend = []

        def flush_one():
            h, jm, et, ek, started = pend.pop(0)
            for qi, (q0, rows) in enumerate(QT):
                ap_, ak_ = accap(qi, rows)
                st = ak_ not in started
                started.add(ak_)
                self.mm(ap_, et[:, q0:q0 + rows], self.mva[ms][:, jm, h, 0:129], st, jm == 1, r=[ek, ('mva', ms)], w=[ak_])
            if jm == 1:
                post(h)

        for h in range(4):
            started = set()
            for jm in range(2):
                pt, pk = self.ring()
                self.mm(pt[:, 0:T], self.mkT[ms][:, h, jm * 128:(jm + 1) * 128], self.qT[:, 8 + h, 0:T], True, True,
                        r=[('mkT', ms), ('qT', 8 + h)], w=[pk])
                et, ek = self.etile()
                self.act(et[:, 0:T], pt[:, 0:T], AF.Exp, r=[pk], w=[ek], scale=sc)
                pend.append((h, jm, et, ek, started))
                if len(pend) > LA:
                    flush_one()
        while pend:
            flush_one()

    def load_sample_caches(self):
        for t in range(2):
            st, sk = self.stg()
            self.dma_in(st, self.cmk[t * 128:(t + 1) * 128, :], w=[sk], dsem='l_' + sk[0] + str(sk[1]))
            self.k_to_mkT(st, sk, 0, t)
            st, sk = self.stg()
            self.dma_in(st, self.cmv[t * 128:(t + 1) * 128, :], w=[sk], dsem='l_' + sk[0] + str(sk[1]))
            self.cp(self.mva[0][:, t, :, 0:128], st.rearrange("p (h d) -> p h d", h=4), r=[sk], w=[('mva', 0)])
        for t in range(4):
            st, sk = self.stg()
            self.dma_in(st, self.cak[t * 128:(t + 1) * 128, :], w=[sk], dsem='l_' + sk[0] + str(sk[1]))
            self.tm_to_fm(st, sk, self.akT[:, :, t * 128:(t + 1) * 128], [('akT', t)], 128)
            st, sk = self.stg()
            self.dma_in(st, self.cav[t * 128:(t + 1) * 128, :], w=[sk], dsem='l_' + sk[0] + str(sk[1]))
            self.cp(self.ava[:, t, :, 0:64], st.rearrange("p (h d) -> p h d", h=8), r=[sk], w=[('ava', t)])
        for j in range(32):
            st, sk = self.stg()
            self.dma_in(st, self.cbk[j * 128:(j + 1) * 128, :], w=[sk], dsem='l_' + sk[0] + str(sk[1]))
            self.tm_to_fm(st, sk, self.bkT[:, :, j * 128:(j + 1) * 128], [('bkT', j)], 128)
            st, sk = self.stg()
            self.dma_in(st, self.cbv[j * 128:(j + 1) * 128, :], w=[sk], dsem='l_' + sk[0] + str(sk[1]))
            self.cp(self.bva[:, j, :, 0:128], st.rearrange("p (h d) -> p h d", h=4), r=[sk], w=[('bva', j)])

    def build(self, limit=None):
        self.declare()
        self.P.tag = 'pro'
        self.prologue_consts()
        self.prologue_ebm()
        self.P.tag = 'cast'
        early = ['wm0', 'wm1', 'aq', 'ak', 'mq', 'av', 'bv', 'bq', 'bk', 'az', 'bz', 'mz']
        self.late_casts = ['m%d' % t for t in range(8)] + ['wo0', 'wo1']
        if limit != 'consts':
            for name in early:
                self.cast_item(name)
        if limit not in ('consts', 'cast'):
            self.P.tag = 'mem'
            self.memory_kv()
        if limit in ('cast', 'mem'):
            for name in self.late_casts:
                self.cast_item(name)
            self.late_casts = []
        if limit in ('consts', 'cast', 'mem'):
            self.P.tag = 'pro'
            self.prologue_ebm_b()
        if limit not in ('consts', 'cast', 'mem'):
            cfgs = []
            if isinstance(limit, list):
                cfgs = [dict(kind='p', s=s_, b=b_) for (s_, b_) in limit]
            else:
                for s in range(2):
                    for b in range(NBLK):
                        if isinstance(limit, int) and len(cfgs) >= limit:
                            continue
                        cfgs.append(dict(kind='p', s=s, b=b))
                if limit is None or limit == 'sample':
                    cfgs.append(dict(kind='s'))
            for i, c in enumerate(cfgs):
                c['next'] = cfgs[i + 1] if i + 1 < len(cfgs) else None
            self.P.tag = 'A1'
            self.a1(self.xsrc_of(cfgs[0]))
            self.P.tag = 'pro'
            self.prologue_ebm_b()
            for c in cfgs:
                if c['kind'] == 's':
                    self.P.tag = 'sload'
                    self.load_sample_caches()
                self.block(c)
        self.P.op('sp', lambda e: None, extra_deps=list(self.out_dmas))
        self.P.emit()


_CACHE = {}


def _get_program():
    if 'nc' not in _CACHE:
        nc = bass.Bass("TRN2", target_bir_lowering=False)
        bld = Builder(nc)
        bld.build()
        _CACHE['nc'] = nc
    return _CACHE['nc']


def _rope_table():
    d = 64
    inv = (1.0 / (np.float32(10000.0) ** (np.arange(0, d, 2, dtype=np.float32) / np.float32(d)))).astype(np.float32)
    pos = np.arange(NPOS, dtype=np.float32)
    ang = (pos[:, None] * inv[None, :]).astype(np.float32)
    return np.concatenate([np.cos(ang), np.sin(ang)], axis=1).astype(np.float32)


def kernel(x_prompt, x_sample, cache_a_k, cache_a_v, cache_b_k, cache_b_v, cache_mem_k, cache_mem_v,
           mem_prompt, norm_in, w_in, rel_bias, lambda_q1, lambda_k1, lambda_q2, lambda_k2, subln,
           norm_mem, w_mem_kv, w_branch_a, w_branch_b, w_branch_m, w_out, norm_final):
    f = lambda a: np.ascontiguousarray(np.asarray(a, dtype=np.float32))
    nc = _get_program()
    ident = np.eye(128, dtype=np.float32)
    cs = _rope_table()
    lam4 = np.concatenate([f(lambda_q1)[0], f(lambda_k1)[0], f(lambda_q2)[0], f(lambda_k2)[0]]).astype(np.float32)
    shared = dict(norm_in=f(norm_in)[0], w_in=f(w_in)[0], rel_bias=f(rel_bias)[0], lam4=lam4, subln=f(subln)[0],
                  norm_mem=f(norm_mem)[0], w_mem=f(w_mem_kv)[0], w_ba=f(w_branch_a)[0], w_bb=f(w_branch_b)[0],
                  w_bm=f(w_branch_m)[0], w_out=f(w_out)[0], norm_final=f(norm_final), ident=ident, cs=cs,
                  jrev=np.ascontiguousarray(ident[::-1]))
    xp = f(x_prompt)
    xs = f(x_sample)
    in_maps = []
    for c in range(NCORES):
        m = dict(shared)
        m['xp'] = xp[2 * c:2 * c + 2]
        m['xs'] = xs[c]
        m['cak'] = f(cache_a_k)[0, c].reshape(512, 512)
        m['cav'] = f(cache_a_v)[0, c].reshape(512, 512)
        m['cbk'] = f(cache_b_k)[0, c].reshape(S, 512)
        m['cbv'] = f(cache_b_v)[0, c].reshape(S, 512)
        m['cmk'] = f(cache_mem_k)[0, c].reshape(256, 512)
        m['cmv'] = f(cache_mem_v)[0, c].reshape(256, 512)
        m['memp'] = f(mem_prompt)[2 * c:2 * c + 2]
        in_maps.append(m)
    res = run_bass_kernel_spmd(nc, in_maps, core_ids=list(range(NCORES)))
    R = res.results
    cat = lambda k: np.concatenate([np.asarray(R[c][k], dtype=np.float32) for c in range(NCORES)], axis=0)
    stk = lambda k: np.stack([np.asarray(R[c][k], dtype=np.float32) for c in range(NCORES)], axis=0)
    y_prompt = cat('yp')
    y_sample = stk('ys')
    nak_p = cat('nak_p').reshape(1, 16, 512, 8, 64)
    nav_p = cat('nav_p').reshape(1, 16, 512, 8, 64)
    nbk_p = cat('nbk_p').reshape(1, 16, S, 8, 64)
    nbv_p = cat('nbv_p').reshape(1, 16, S, 4, 128)
    nmk_p = cat('nmk_p').reshape(1, 16, 256, 4, 128)
    nmv_p = cat('nmv_p').reshape(1, 16, 256, 4, 128)
    nak_s = stk('nak_s').reshape(1, 8, 16, 8, 64)
    nav_s = stk('nav_s').reshape(1, 8, 16, 8, 64)
    nbk_s = stk('nbk_s').reshape(1, 8, 16, 8, 64)
    nbv_s = stk('nbv_s').reshape(1, 8, 16, 4, 128)
    return (y_prompt, y_sample, nak_p, nav_p, nbk_p, nbv_p, nmk_p, nmv_p, nak_s, nav_s, nbk_s, nbv_s)
```
